# Optimizing a Trainium2 kernel written in Bass

```python
import math
import jax, jax.numpy as jnp
from jax import lax
import numpy as np

D_MODEL = 1024
BATCH = 16
SEQ = 2048
DEPTH = 2

N_EVEN = (DEPTH + 1) // 2
N_ODD = DEPTH // 2

A_WIDTH = D_MODEL // 2
A_EXPAND = 128
A_HEADS = A_WIDTH // A_EXPAND
A_DK = A_EXPAND
A_DV = A_WIDTH // A_HEADS
A_CHUNK = 64
B_WIDTH = D_MODEL - A_WIDTH
B_GROUPS = 4
B_GROUP_DIM = B_WIDTH // B_GROUPS
B_CHUNK = 128
C_HEADS = 16
C_HEAD_DIM = D_MODEL // C_HEADS
C_ROT_DIM = C_HEAD_DIM // 4
ROPE_THETA = 500000.0
C_BRANCHES = ((128, 1), (512, 4), (2048, 16))
C_BLOCK = 128
D_FF = 4 * D_MODEL
EPS = 1e-6

EVEN_IN = 4 * A_WIDTH + 2 * B_WIDTH
ODD_IN = 3 * D_MODEL

kernel_name = "hybrid_hgrn2_gmlp_dilated_attn"

F32 = jnp.float32


def rmsnorm(x, g):
    xf = x.astype(F32)
    y = xf * lax.rsqrt(jnp.mean(xf * xf, axis=-1, keepdims=True) + EPS)
    return (y * g.astype(F32)).astype(x.dtype)


def layernorm(x, g, b):
    xf = x.astype(F32)
    mu = jnp.mean(xf, axis=-1, keepdims=True)
    var = jnp.mean(jnp.square(xf - mu), axis=-1, keepdims=True)
    return ((xf - mu) * lax.rsqrt(var + EPS) * g.astype(F32) + b.astype(F32)).astype(x.dtype)


def hgrn2_mix(q, f_logit, i, g, lb, norm_g):
    b_, s_, _ = q.shape
    n_chunks = s_ // A_CHUNK
    f = lb[None, None, :] + (1.0 - lb[None, None, :]) * jax.nn.sigmoid(f_logit.astype(F32))
    k = 1.0 - f
    logf = jnp.log(f)
    qf = jax.nn.silu(q.astype(F32))

    def chunks(t, d):
        return t.reshape(b_, n_chunks, A_CHUNK, A_HEADS, d).transpose(1, 0, 3, 2, 4)

    xs = (chunks(qf, A_DK), chunks(k, A_DK), chunks(i.astype(F32), A_DV), chunks(logf, A_DK))
    causal = np.tril(np.ones((A_CHUNK, A_CHUNK), dtype=bool))

    def step(state, inp):
        qc, kc, vc, lfc = inp
        G = jnp.cumsum(lfc, axis=2)
        o_inter = jnp.einsum('bhtk,bhkv->bhtv', qc * jnp.exp(G), state)
        diff = G[:, :, :, None, :] - G[:, :, None, :, :]
        decay = jnp.exp(jnp.where(causal[None, None, :, :, None], diff, -jnp.inf))
        scores = jnp.einsum('bhtk,bhsk,bhtsk->bhts', qc, kc, decay)
        o_intra = jnp.einsum('bhts,bhsv->bhtv', scores, vc)
        G_last = G[:, :, -1:, :]
        new_state = jnp.exp(G_last[:, :, 0, :])[..., None] * state + jnp.einsum(
            'bhsk,bhsv->bhkv', kc * jnp.exp(G_last - G), vc)
        return new_state, o_inter + o_intra

    state0 = jnp.zeros((b_, A_HEADS, A_DK, A_DV), F32)
    _, o = lax.scan(step, state0, xs)
    o = o.transpose(1, 0, 3, 2, 4).reshape(b_, s_, A_HEADS, A_DV)
    o = rmsnorm(o, norm_g.reshape(A_HEADS, A_DV))
    o = o.reshape(b_, s_, A_WIDTH) * jax.nn.silu(g.astype(F32))
    return o.astype(q.dtype)


def chunk_gmlp_mix(u, v, ln_g, ln_b, w_s, b_s):
    b_, s_, _ = u.shape
    v = layernorm(v, ln_g, ln_b)
    vb = v.reshape(b_, s_ // B_CHUNK, B_CHUNK, B_GROUPS, B_GROUP_DIM)
    tril = np.tril(np.ones((B_CHUNK, B_CHUNK), dtype=bool))
    w = jnp.where(tril[None], w_s, jnp.zeros_like(w_s))
    mixed = jnp.einsum('gts,bnsgc->bntgc', w, vb) + b_s.T[None, None, :, :, None]
    return u * mixed.reshape(b_, s_, B_WIDTH).astype(u.dtype)


def rope_partial(x, pos):
    half = C_ROT_DIM // 2
    inv = ROPE_THETA ** (-jnp.arange(half, dtype=F32) / half)
    ang = pos[..., None].astype(F32) * inv
    cos, sin = jnp.cos(ang)[:, :, None, :], jnp.sin(ang)[:, :, None, :]
    xf = x.astype(F32)
    x1, x2, xp = xf[..., :half], xf[..., half:C_ROT_DIM], xf[..., C_ROT_DIM:]
    return jnp.concatenate([x1 * cos - x2 * sin, x1 * sin + x2 * cos, xp], axis=-1)


def dilated_branch(q, k, v, window, dilation):
    b_, h_, s_, dh = q.shape
    L = s_ // dilation
    W = window // dilation
    qb_len = min(C_BLOCK, L)
    n_blk = L // qb_len

    def sub(t):
        return t.reshape(b_, h_, L, dilation, dh).transpose(0, 1, 3, 2, 4)

    qs = sub(q).reshape(b_, h_, dilation, n_blk, qb_len, dh)
    pad = ((0, 0), (0, 0), (0, 0), (W, 0), (0, 0))
    kp, vp = jnp.pad(sub(k), pad), jnp.pad(sub(v), pad)
    idx = np.arange(n_blk)[:, None] * qb_len + np.arange(qb_len + W)[None, :]
    kb = jnp.take(kp, idx, axis=3)
    vb = jnp.take(vp, idx, axis=3)
    q_pos = np.arange(n_blk)[:, None] * qb_len + np.arange(qb_len)[None, :]
    k_pos = idx - W
    dist = q_pos[:, :, None] - k_pos[:, None, :]
    mask = (dist >= 0) & (dist <= W) & (k_pos[:, None, :] >= 0)
    s = jnp.einsum('bhrnqd,bhrnkd->bhrnqk', qs, kb) * (1.0 / math.sqrt(dh))
    s = jnp.where(mask, s, -jnp.inf)
    m = jnp.max(s, axis=-1, keepdims=True)
    p = jnp.exp(s - m)
    den = jnp.sum(p, axis=-1, keepdims=True)
    o = jnp.einsum('bhrnqk,bhrnkd->bhrnqd', p, vb) / den

    def unsub(t):
        c = t.shape[-1]
        return t.reshape(b_, h_, dilation, L, c).transpose(0, 1, 3, 2, 4).reshape(b_, h_, s_, c)

    return unsub(o), unsub(m), unsub(den)


def dilated_attention_mix(h, w_in, pos):
    b_, s_, _ = h.shape
    qkv = h @ w_in
    q, k, v = jnp.split(qkv, 3, axis=-1)
    q = rope_partial(q.reshape(b_, s_, C_HEADS, C_HEAD_DIM), pos).transpose(0, 2, 1, 3)
    k = rope_partial(k.reshape(b_, s_, C_HEADS, C_HEAD_DIM), pos).transpose(0, 2, 1, 3)
    v = v.reshape(b_, s_, C_HEADS, C_HEAD_DIM).transpose(0, 2, 1, 3).astype(F32)
    outs, maxes, dens = [], [], []
    for window, dilation in C_BRANCHES:
        o_g, m_g, d_g = dilated_branch(q, k, v, window, dilation)
        outs.append(o_g); maxes.append(m_g); dens.append(d_g)
    m_all = jnp.max(jnp.stack(maxes, 0), axis=0)
    weights = [d_g * jnp.exp(m_g - m_all) for m_g, d_g in zip(maxes, dens)]
    o = sum(w_g * o_g for w_g, o_g in zip(weights, outs)) / sum(weights)
    return o.transpose(0, 2, 1, 3).reshape(b_, s_, D_MODEL).astype(h.dtype)


def setup_inputs(seed: int = 0) -> dict:
    key = jax.random.key(seed)
    ks = jax.random.split(key, 20)

    def nrm(k, shape, scale):
        return jax.random.normal(k, shape, F32) * scale

    def gain(k, shape):
        return 1.0 + 0.05 * jax.random.normal(k, shape, F32)

    return {
        'x': nrm(ks[0], (BATCH, SEQ, D_MODEL), 1.0),
        'positions': jnp.broadcast_to(jnp.arange(SEQ, dtype=jnp.int32), (BATCH, SEQ)),
        'norm_mix_pre': gain(ks[1], (DEPTH, D_MODEL)),
        'norm_mix_post': gain(ks[2], (DEPTH, D_MODEL)),
        'norm_ffn_pre': gain(ks[3], (DEPTH, D_MODEL)),
        'norm_ffn_post': gain(ks[4], (DEPTH, D_MODEL)),
        'w_in_even': nrm(ks[5], (N_EVEN, D_MODEL, EVEN_IN), D_MODEL ** -0.5),
        'lb_table': nrm(ks[6], (DEPTH + 1, A_WIDTH), 0.5),
        'a_norm': gain(ks[7], (N_EVEN, A_WIDTH)),
        'b_ln_g': gain(ks[8], (N_EVEN, B_WIDTH)),
        'b_ln_b': nrm(ks[9], (N_EVEN, B_WIDTH), 0.02),
        'b_ws': nrm(ks[10], (N_EVEN, B_GROUPS, B_CHUNK, B_CHUNK), B_CHUNK ** -0.5),
        'b_bias': 1.0 + nrm(ks[11], (N_EVEN, B_GROUPS, B_CHUNK), 0.1),
        'w_out_even': nrm(ks[12], (N_EVEN, A_WIDTH + B_WIDTH, D_MODEL), (A_WIDTH + B_WIDTH) ** -0.5),
        'w_in_odd': nrm(ks[13], (N_ODD, D_MODEL, ODD_IN), D_MODEL ** -0.5),
        'w_out_odd': nrm(ks[14], (N_ODD, D_MODEL, D_MODEL), D_MODEL ** -0.5),
        'w_ff1': nrm(ks[15], (DEPTH, D_MODEL, D_FF), D_MODEL ** -0.5),
        'w_ff2': nrm(ks[16], (DEPTH, D_FF, D_MODEL), D_FF ** -0.5),
    }


def reference(x, positions, norm_mix_pre, norm_mix_post, norm_ffn_pre, norm_ffn_post,
              w_in_even, lb_table, a_norm, b_ln_g, b_ln_b, b_ws, b_bias, w_out_even,
              w_in_odd, w_out_odd, w_ff1, w_ff2):
    lb_all = jnp.cumsum(jax.nn.softmax(lb_table.astype(F32), axis=0), axis=0)
    splits = [A_WIDTH, 2 * A_WIDTH, 3 * A_WIDTH, 4 * A_WIDTH, 4 * A_WIDTH + B_WIDTH]
    for l in range(DEPTH):
        h = rmsnorm(x, norm_mix_pre[l])
        if l % 2 == 0:
            e = l // 2
            proj = h @ w_in_even[e]
            qa, fa, ia, ga, ub, vb = jnp.split(proj, splits, axis=-1)
            oa = hgrn2_mix(qa, fa, ia, ga, lb_all[l], a_norm[e])
            ob = chunk_gmlp_mix(jax.nn.gelu(ub), jax.nn.gelu(vb), b_ln_g[e], b_ln_b[e], b_ws[e], b_bias[e])
            mix = jnp.concatenate([oa, ob], axis=-1) @ w_out_even[e]
        else:
            o = l // 2
            mix = dilated_attention_mix(h, w_in_odd[o], positions) @ w_out_odd[o]
        x = x + rmsnorm(mix, norm_mix_post[l])
        h = rmsnorm(x, norm_ffn_pre[l])
        y = jnp.square(jax.nn.relu(h @ w_ff1[l])) @ w_ff2[l]
        x = x + rmsnorm(y, norm_ffn_post[l])
    return x
```

```python
import math
import numpy as np
from contextlib import ExitStack
import concourse.bass as bass
import concourse.mybir as mybir
from concourse.bass_utils import run_bass_kernel_spmd

F32 = mybir.dt.float32
BF16 = mybir.dt.bfloat16
I32 = mybir.dt.int32
ALU = mybir.AluOpType
AF = mybir.ActivationFunctionType

N_DMA_SEMS = 24
STRICT_SAME_ENGINE = False
D = 1024
S = 2048
NC8 = 8
EPS = 1e-6
DFF = 4096
ROPE_THETA = 500000.0


class Ins:
    __slots__ = ("eng", "fn", "deps", "sig", "seq", "is_dma", "dsem", "dval", "idx")

    def __init__(self, eng, fn, is_dma=False):
        self.eng = eng
        self.fn = fn
        self.deps = []
        self.sig = False
        self.seq = 0
        self.is_dma = is_dma
        self.dsem = None
        self.dval = 0
        self.idx = 0


class Buf:
    __slots__ = ("name", "last_w", "readers", "excl")

    def __init__(self, name="", excl=False):
        self.name = name
        self.last_w = None
        self.readers = {}
        self.excl = excl


class Prog:
    def __init__(self, nc):
        self.nc = nc
        self.streams = {"pe": [], "act": [], "dve": [], "pool": [], "sp": []}
        self.n_dma = 0
        self.dma_list = []
        self.out_dmas = []

    def _dep(self, ins, prod, kind):
        if prod is None or prod is ins:
            return
        if not prod.is_dma and not ins.is_dma and prod.eng == ins.eng:
            if ins.eng == "pe" or (kind != "RAW" and not STRICT_SAME_ENGINE):
                return
        if prod not in ins.deps:
            ins.deps.append(prod)
            prod.sig = True

    def op(self, eng, fn, reads=(), writes=(), is_dma=False):
        ins = Ins(eng, fn, is_dma)
        ex = [b for b in reads if b.excl]
        if ex:
            reads = [b for b in reads if not b.excl]
            writes = list(writes) + [b for b in ex if b not in writes]
        for b in reads:
            self._dep(ins, b.last_w, "RAW")
        for b in writes:
            self._dep(ins, b.last_w, "WAW")
            for r in b.readers.values():
                self._dep(ins, r, "WAR")
        for b in reads:
            b.readers[("dma", id(ins)) if is_dma else eng] = ins
        for b in writes:
            b.last_w = ins
            b.readers = {}
        self.streams[eng].append(ins)
        if is_dma:
            ins.idx = self.n_dma
            self.n_dma += 1
            self.dma_list.append(ins)
            ins.sig = True
        return ins

    def I(self, eng, name, reads=(), writes=(), **kw):
        return self.op(eng, lambda e: getattr(e, name)(**kw), reads, writes)

    def dma(self, out, in_, reads=(), writes=(), eng="sp", is_output=False, **kw):
        ins = self.op(eng, lambda e: e.dma_start(out=out, in_=in_, **kw), reads, writes, is_dma=True)
        if is_output:
            self.out_dmas.append(ins)
        return ins

    def barrier(self):
        drains = {}
        for e in self.streams:
            ins = Ins(e, lambda eng: eng.drain())
            ins.sig = True
            self.streams[e].append(ins)
            drains[e] = ins
        dmas = list(self.dma_list[getattr(self, "_bar_dma", 0):])
        self._bar_dma = len(self.dma_list)
        for e in self.streams:
            w = Ins(e, None)
            w.deps = [d for k, d in drains.items() if k != e] + dmas
            self.streams[e].append(w)

    def emit(self):
        nc = self.nc
        with ExitStack() as st:
            esem = {e: st.enter_context(nc.semaphore("s_" + e)) for e in self.streams}
            dsems = [st.enter_context(nc.semaphore("d%d" % i)) for i in range(N_DMA_SEMS)]
            dcount = [0] * N_DMA_SEMS
            for e, lst in self.streams.items():
                c = 0
                for ins in lst:
                    if not ins.is_dma and ins.sig:
                        c += 1
                        ins.seq = c
            prev_on_sem = {}
            half = N_DMA_SEMS // 2
            cnt = {"sp": 0, "pool": 0}
            for ins in self.dma_list:
                q = "pool" if ins.eng == "pool" else "sp"
                j = (cnt[q] % half) + (half if q == "pool" else 0)
                cnt[q] += 1
                dcount[j] += 16
                ins.dsem = j
                ins.dval = dcount[j]
                p = prev_on_sem.get(j)
                if p is not None and p not in ins.deps:
                    ins.deps.append(p)
                prev_on_sem[j] = ins
            fin = Ins("sp", None)
            fin.deps = list(self.out_dmas)
            self.streams["sp"].append(fin)
            blk = st.enter_context(nc.Block())

            def mk(ename):
                lst = self.streams[ename]

                def body(eng):
                    known = {}
                    for ins in lst:
                        for d in ins.deps:
                            if d.is_dma:
                                key, val, sem = ("d", d.dsem), d.dval, dsems[d.dsem]
                            else:
                                key, val, sem = ("e", d.eng), d.seq, esem[d.eng]
                            if known.get(key, 0) >= val:
                                continue
                            eng.wait_ge(sem, val)
                            known[key] = val
                        if ins.fn is None:
                            continue
                        r = ins.fn(eng)
                        if ins.is_dma:
                            r.then_inc(dsems[ins.dsem], 16)
                        elif ins.sig:
                            r.then_inc(esem[ename], 1)
                return body

            blk.tensor(mk("pe"))
            blk.scalar(mk("act"))
            blk.vector(mk("dve"))
            blk.gpsimd(mk("pool"))
            blk.sync(mk("sp"))


class Ring:
    def __init__(self, items):
        self.items = items
        self.i = 0

    def next(self):
        it = self.items[self.i % len(self.items)]
        self.i += 1
        return it


def build(nseq=2, cfg=None):
    cfg = cfg or {}
    do_l0mix = cfg.get("l0mix", True)
    do_l0ffn = cfg.get("l0ffn", True)
    do_l1mix = cfg.get("l1mix", True)
    do_l1ffn = cfg.get("l1ffn", True)

    nc = bass.Bass("TRN2", target_bir_lowering=False)
    dr = lambda n, s, dt=F32, kind="ExternalInput": nc.dram_tensor(n, s, dt, kind=kind).ap()
    x_d = dr("x", [nseq * S, D])
    pos_d = dr("positions", [nseq * S], I32)
    g_d = {k: dr(k, [2, D]) for k in ("norm_mix_pre", "norm_mix_post", "norm_ffn_pre", "norm_ffn_post")}
    w_in_even = dr("w_in_even", [D, 3072])
    lb_table = dr("lb_table", [3, 512])
    a_norm = dr("a_norm", [512])
    b_ln_g = dr("b_ln_g", [1, 512])
    b_ln_b = dr("b_ln_b", [1, 512])
    b_ws = dr("b_ws", [4, 128, 128])
    b_bias = dr("b_bias", [1, 512])
    w_out_even = dr("w_out_even", [D, D])
    w_in_odd = dr("w_in_odd", [D, 3072])
    w_out_odd = dr("w_out_odd", [D, D])
    w_ff1 = dr("w_ff1", [2, D, DFF])
    w_ff2 = dr("w_ff2", [2, DFF, D])
    out_d = dr("out", [nseq * S, D], kind="ExternalOutput")

    P = Prog(nc)
    st = ExitStack()
    with st:
        def sb(name, shape, dt=F32):
            return st.enter_context(nc.sbuf_tensor(name, shape, dt))

        xT = sb("xT", [128, 8, S])
        XT = [[Buf() for _ in range(4)] for _ in range(8)]
        cst_d = dr("cst", [128, 904])
        am_d = dr("amask", [128, 9, 512])
        ident_f = sb("ident_f", [128, 128])
        ident_b = sb("ident_b", [128, 128], BF16)
        ones_b = sb("ones_b", [128, 128], BF16)
        ones_f = sb("ones_f", [128, 128])
        gains = sb("gains", [128, 4, 2, 8])
        epsb = sb("epsb", [128, 1])
        CONST = Buf("const")
        GN = {"norm_mix_pre": 0, "norm_mix_post": 1, "norm_ffn_pre": 2, "norm_ffn_post": 3}

        ps_t = [st.enter_context(nc.psum_tensor("ps%d" % i, [128, 512], F32)) for i in range(8)]
        PS = [Buf("ps%d" % i, excl=True) for i in range(8)]
        psring = Ring([(ps_t[i], PS[i]) for i in range(6)])

        P.op("pool", lambda e: e.memset(ones_b[:], 1.0), writes=[CONST])
        P.op("pool", lambda e: e.memset(ones_f[:], 1.0), writes=[CONST])
        P.op("pool", lambda e: e.memset(epsb[:], EPS), writes=[CONST])
        P.dma(ident_f[:], cst_d[:, 0:128], writes=[CONST])
        P.op("pool", lambda e: e.tensor_copy(out=ident_b[:], in_=ident_f[:]), reads=[CONST], writes=[CONST])
        gstage = sb("gstage", [64, 128])
        GSB = Buf()
        for k, gi in GN.items():
            P.dma(gstage[gi * 16:(gi + 1) * 16, :], g_d[k].rearrange("l (c p) -> (l c) p", p=128), writes=[GSB])
        P.op("pe", lambda e: e.transpose(out=ps_t[0][:, 0:64], in_=gstage[:, :], identity=ident_f[0:64, 0:64]),
             reads=[GSB, CONST], writes=[PS[0]])
        P.op("dve", lambda e: e.tensor_copy(out=gains[:].rearrange("p a b c -> p (a b c)"), in_=ps_t[0][:, 0:64]),
             reads=[PS[0]], writes=[CONST])

        wdma_eng = "pool"

        def load_w(dst_ap, src_ap, buf):
            return P.dma(dst_ap, src_ap, writes=[buf], eng=wdma_eng)

        sqs_t = [sb("sqs%d" % i, [128, 1, 512], BF16) for i in range(3)]
        sqsring = Ring([(sqs_t[i], Buf()) for i in range(3)])
        rstd_t = [sb("rstd%d" % i, [128, 512]) for i in range(3)]
        rsring = Ring([(rstd_t[i], Buf()) for i in range(3)])

        def rstd_from_psum(pst, PSB, n_feat):
            rt, RB = rsring.next()
            P.op("act", lambda e: e.activation(out=rt[:], in_=pst[:], func=AF.Ln, scale=1.0 / n_feat, bias=epsb[:, 0:1]),
                 reads=[PSB, CONST], writes=[RB])
            P.op("act", lambda e: e.activation(out=rt[:], in_=rt[:], func=AF.Exp, scale=-0.5), reads=[RB], writes=[RB])
            return rt, RB

        def prenorm(t0, gi, layer, hT, HB, hoff, sqbuf=None):
            ti = t0 // 512
            sq, SQB = sqbuf
            xs = [XT[c][ti] for c in range(8)]
            P.op("act", lambda e: e.activation(out=sq[:], in_=xT[:, :, t0:t0 + 512], func=AF.Square), reads=xs, writes=[SQB])
            pst, PSB = psring.next()
            for c in range(8):
                P.op("pe", lambda e, c=c: e.matmul(pst[:], lhsT=ones_b[:], rhs=sq[:, c, :], start=(c == 0), stop=(c == 7)),
                     reads=[SQB, CONST], writes=[PSB])
            rt, RB = rstd_from_psum(pst, PSB, D)
            for c in range(8):
                P.op("dve", lambda e, c=c: e.scalar_tensor_tensor(
                    out=hT[:, c, hoff:hoff + 512], in0=xT[:, c, t0:t0 + 512], scalar=gains[:, gi, layer, c:c + 1],
                    in1=rt[:], op0=ALU.mult, op1=ALU.mult), reads=[XT[c][ti], RB, CONST], writes=[HB])

        def ffn(layer, pools):
            h2, H2B, hid, HIDB, w1r, w2r, sqbuf, YTB, rl_ring = pools
            for tt in range(2):
                T0 = tt * 1024
                for sub in range(2):
                    prenorm(T0 + sub * 512, GN["norm_ffn_pre"], layer, h2, H2B[sub], sub * 512, sqbuf=sqbuf)
                for hg in range(8):
                    w1, W1B = w1r.next()
                    load_w(w1[:], w_ff1[layer].rearrange("(kc p) n -> p kc n", p=128)[:, :, hg * 512:(hg + 1) * 512], W1B)
                    for sub, hh in [(s_, h_) for s_ in range(2) for h_ in range(4)]:
                        hc = hg * 4 + hh
                        if True:
                            pst, PSB = psring.next()
                            for kc in range(8):
                                P.op("pe", lambda e, kc=kc, hh=hh, sub=sub, pst=pst, w1=w1: e.matmul(
                                    pst[:], lhsT=w1[:, kc, hh * 128:(hh + 1) * 128], rhs=h2[:, kc, sub * 512:(sub + 1) * 512],
                                    start=(kc == 0), stop=(kc == 7)), reads=[W1B, H2B[sub]], writes=[PSB])
                            rl, RLB = rl_ring.next()
                            P.op("act", lambda e, pst=pst, rl=rl: e.activation(out=rl[:], in_=pst[:], func=AF.Relu),
                                 reads=[PSB], writes=[RLB])
                            P.op("dve", lambda e, rl=rl, hc=hc, sub=sub: e.tensor_tensor(
                                out=hid[:, hc, sub * 512:(sub + 1) * 512], in0=rl[:], in1=rl[:], op=ALU.mult),
                                reads=[RLB], writes=[HIDB[hc]])
                st_ps = [(ps_t[6], PS[6]), (ps_t[7], PS[7])]
                for oc in range(8):
                    w2, W2B = w2r.next()
                    load_w(w2[:], w_ff2[layer].rearrange("(hc p) n -> p hc n", p=128)[:, :, oc * 128:(oc + 1) * 128], W2B)
                    for sub in range(2):
                        pst, PSB = psring.next()
                        for hc in range(32):
                            P.op("pe", lambda e, hc=hc, sub=sub, pst=pst, w2=w2: e.matmul(
                                pst[:], lhsT=w2[:, hc, :], rhs=hid[:, hc, sub * 512:(sub + 1) * 512],
                                start=(hc == 0), stop=(hc == 31)), reads=[W2B, HIDB[hc]], writes=[PSB])
                        sq, SQB = sqsring.next()
                        P.op("act", lambda e, pst=pst, sq=sq: e.activation(out=sq[:, 0, :], in_=pst[:], func=AF.Square),
                             reads=[PSB], writes=[SQB])
                        P.op("dve", lambda e, pst=pst, oc=oc, sub=sub: e.tensor_copy(
                            out=h2[:, oc, sub * 512:(sub + 1) * 512], in_=pst[:]), reads=[PSB], writes=[H2B[sub]])
                        sp_, SPB = st_ps[sub]
                        P.op("pe", lambda e, sq=sq, sp_=sp_, oc=oc: e.matmul(sp_[:], lhsT=ones_b[:], rhs=sq[:, 0, :],
                                                                      start=(oc == 0), stop=(oc == 7)),
                             reads=[SQB, CONST], writes=[SPB])
                for sub in range(2):
                    sp_, SPB = st_ps[sub]
                    rt, RB = rstd_from_psum(sp_, SPB, D)
                    t0 = T0 + sub * 512
                    ti = t0 // 512
                    for c in range(8):
                        yv = h2[:, c, sub * 512:(sub + 1) * 512]
                        tm, TMB = rl_ring.next()
                        P.op("dve", lambda e, c=c, yv=yv, rt=rt, tm=tm: e.scalar_tensor_tensor(
                            out=tm[:], in0=yv, scalar=gains[:, GN["norm_ffn_post"], layer, c:c + 1], in1=rt[:],
                            op0=ALU.mult, op1=ALU.mult), reads=[H2B[sub], RB, CONST], writes=[TMB])
                        P.op("dve", lambda e, c=c, tm=tm, t0=t0: e.tensor_tensor(
                            out=xT[:, c, t0:t0 + 512], in0=xT[:, c, t0:t0 + 512], in1=tm[:], op=ALU.add),
                            reads=[TMB, XT[c][ti]], writes=[XT[c][ti]])

        TWO_PI = 2.0 * math.pi

        def l0mix(sq_i):
            with ExitStack() as ph:
                def en(n, s, dt=F32):
                    return ph.enter_context(nc.sbuf_tensor("%s_s%d" % (n, sq_i), s, dt)), Buf(n)
                DV = lambda name, reads, writes, **kw: P.I("dve", name, reads=reads, writes=writes, **kw)
                AC = lambda reads, writes, **kw: P.I("act", "activation", reads=reads, writes=writes, **kw)
                MM = lambda reads, writes, **kw: P.I("pe", "matmul", reads=reads, writes=writes, **kw)
                sq, SQB = en("l0sq", [128, 8, 512], BF16)
                cmask, CMB = en("cmask", [128, 512])
                tmask, TMB = en("tmask", [128, 256])
                P.dma(cmask[:], cst_d[:, 128:640], writes=[CMB])
                P.dma(tmask[:], cst_d[:, 640:896], writes=[TMB])
                l0stage, L0S = en("l0stage", [16, 128])
                P.dma(l0stage[0:12, :], lb_table.rearrange("r (h p) -> (r h) p", p=128), writes=[L0S])
                P.dma(l0stage[12:16, :], a_norm.rearrange("(h p) -> h p", p=128), writes=[L0S])
                l0c, L0C = en("l0c", [128, 32])
                pst, PSB = psring.next()
                P.I("pe", "transpose", out=pst[:, 0:16], in_=l0stage[:, :], identity=ident_f[0:16, 0:16], reads=[L0S, CONST], writes=[PSB])
                DV("tensor_copy", [PSB], [L0C], out=l0c[:, 0:16], in_=pst[:, 0:16])
                AC([L0C], [L0C], out=l0c[:, 0:12], in_=l0c[:, 0:12], func=AF.Exp)
                DV("tensor_tensor", [L0C], [L0C], out=l0c[:, 28:32], in0=l0c[:, 0:4], in1=l0c[:, 4:8], op=ALU.add)
                DV("tensor_tensor", [L0C], [L0C], out=l0c[:, 28:32], in0=l0c[:, 28:32], in1=l0c[:, 8:12], op=ALU.add)
                DV("reciprocal", [L0C], [L0C], out=l0c[:, 28:32], in_=l0c[:, 28:32])
                DV("tensor_tensor", [L0C], [L0C], out=l0c[:, 16:20], in0=l0c[:, 0:4], in1=l0c[:, 28:32], op=ALU.mult)
                DV("tensor_scalar", [L0C], [L0C], out=l0c[:, 20:24], in0=l0c[:, 16:20], scalar1=-1.0, scalar2=1.0, op0=ALU.mult, op1=ALU.add)
                DV("tensor_scalar", [L0C], [L0C], out=l0c[:, 24:28], in0=l0c[:, 16:20], scalar1=-1.0, scalar2=None, op0=ALU.add)
                an = lambda hd: l0c[:, 12 + hd:13 + hd]
                lbp = lambda hd: l0c[:, 16 + hd:17 + hd]
                oml = lambda hd: l0c[:, 20 + hd:21 + hd]
                noml = lambda hd: l0c[:, 24 + hd:25 + hd]
                rows, ROWS = en("rows", [1, 1, 512])
                rsrc = (b_ln_g, b_ln_b, b_bias)
                gbc, GBC = en("gbc", [128, 512])
                bbc, BBC = en("bbc", [128, 512])
                biasbc, BIB = en("biasbc", [128, 512])
                for k, (dst, DB) in enumerate(((gbc, GBC), (bbc, BBC), (biasbc, BIB))):
                    P.dma(rows[0:1, 0, :], rsrc[k][:, :], writes=[ROWS])
                    pst, PSB = psring.next()
                    MM([ROWS, CONST], [PSB], out=pst[:], lhsT=ones_f[0:1, :], rhs=rows[0:1, 0, :], start=True, stop=True)
                    DV("tensor_copy", [PSB], [DB], out=dst[:], in_=pst[:])
                wsl, WSL = en("wsl", [128, 4, 128])
                wsT, WST = en("wsT", [128, 4, 128], BF16)
                P.dma(wsl[:], b_ws.rearrange("g t s -> t g s"), writes=[WSL])
                for g in range(4):
                    P.I("pool", "affine_select", reads=[WSL], writes=[WSL], out=wsl[:, g, :], in_=wsl[:, g, :], pattern=[[-1, 128]],
                        compare_op=ALU.is_ge, fill=0.0, base=0, channel_multiplier=1)
                    pst, PSB = psring.next()
                    P.I("pe", "transpose", out=pst[:, 0:128], in_=wsl[:, g, :], identity=ident_f[:], reads=[WSL, CONST], writes=[PSB])
                    DV("tensor_copy", [PSB], [WST], out=wsT[:, g, :], in_=pst[:, 0:128])
                hT, HB = en("hT", [128, 8, 512], BF16)
                wcs = [en("wc%d" % i, [128, 8, 512], BF16) for i in range(2)]
                wring = Ring(wcs)
                mixT, MXB = en("mixT", [128, 8, 512], BF16)
                MXBs = [Buf() for _ in range(8)]
                itok, ITB = en("itok", [128, 4, 512], BF16)
                qs4, QSB = en("qs4", [128, 4, 512], BF16)
                big, _ = en("big", [128, 8, 512])
                sg4, SGB = big[:, 0:4, :], Buf("sg4")
                gate4, GTB = en("gate4", [128, 4, 512], BF16)
                uT, UTB = en("uT", [128, 4, 512], BF16)
                vn, VNB = en("vn", [128, 4, 512], BF16)
                bst, BSTB = en("bst", [128, 8])
                lf, LFB = big[:, 4, :], Buf("lf")
                kk, KKB = big[:, 5, :], Buf("kk")
                vt, VTB = lf, LFB
                vh, VHB = kk, KKB
                G, GB = big[:, 6, :], Buf("G")
                D1, D1B = big[:, 7, :], Buf("D1")
                D3, D3B = D1, D1B
                E1, E1B = en("E1", [128, 512])
                E2, E2B = en("E2", [128, 512])
                E3, E3B = E1, E1B
                qexp, _ = en("qexp", [128, 4, 512], BF16)
                kexp, _ = en("kexp", [128, 4, 512], BF16)
                kdec, _ = en("kdec", [128, 4, 512], BF16)
                kdT, _ = en("kdT", [128, 4, 512], BF16)
                smT, _ = en("smT", [128, 4, 256], BF16)
                eg, _ = en("eg", [128, 4, 16])
                QEBs, KEBs, KDBs, KTBs, SMBs, EGBs = [[Buf() for _ in range(4)] for _ in range(6)]
                Sw, SWB = en("Sw", [128, 9, 128])
                sgs = [en("Sg%d" % i, [128, 8, 128], BF16) for i in range(2)]
                Sc, SCB = en("Sc", [128, 4, 128])
                t1, T1B = en("t1", [128, 512])
                msb, MSB = big, Buf("msb")
                ALIAS = [SGB, LFB, KKB, GB, D1B]
                tm2s = [(t1, T1B), en("tm2b", [128, 512])]
                P.I("pool", "memset", ap=Sc[:], constant=0.0, writes=[SCB])
                w_in_v = w_in_even.rearrange("(kc p) n -> p kc n", p=128)
                w_out_v = w_out_even.rearrange("(kc p) n -> p kc n", p=128)
                G3 = G.rearrange("p (c t) -> p c t", t=64)

                def proj_fm(w, WB, col0, rhsT, RB_):
                    pst, PSB = psring.next()
                    for kc in range(8):
                        MM([WB, RB_], [PSB], out=pst[:], lhsT=w[:, kc, col0:col0 + 128], rhs=rhsT[:, kc, :], start=(kc == 0), stop=(kc == 7))
                    return pst, PSB

                def proj_tm(w, WB, b):
                    pst, PSB = psring.next()
                    for kc in range(8):
                        MM([WB, HB], [PSB], out=pst[:], lhsT=hT[:, kc, b * 128:(b + 1) * 128], rhs=w[:, kc, :], start=(kc == 0), stop=(kc == 7))
                    return pst, PSB

                for ti in range(4):
                    t0 = ti * 512
                    prenorm(t0, GN["norm_mix_pre"], 0, hT, HB, 0, sqbuf=(sq, SQB))
                    def proj_group(g):
                        w, WB = wring.next()
                        load_w(w[:], w_in_v[:, :, g * 512:(g + 1) * 512], WB)
                        if g in (0, 1, 3, 4):
                            for hd in range(4):
                                pst, PSB = proj_fm(w, WB, hd * 128, hT, HB)
                                if g == 0:
                                    AC([PSB], [QSB], out=qs4[:, hd, :], in_=pst[:], func=AF.Silu)
                                elif g == 1:
                                    AC([PSB], [SGB], out=sg4[:, hd, :], in_=pst[:], func=AF.Sigmoid)
                                elif g == 3:
                                    AC([PSB], [GTB], out=gate4[:, hd, :], in_=pst[:], func=AF.Silu)
                                else:
                                    AC([PSB], [UTB], out=uT[:, hd, :], in_=pst[:], func=AF.Gelu)
                        elif g == 2:
                            for b in range(4):
                                pst, PSB = proj_tm(w, WB, b)
                                AC([PSB], [ITB], out=itok[:, b, :], in_=pst[:], func=AF.Copy)
                        else:
                            for b in range(4):
                                pst, PSB = proj_tm(w, WB, b)
                                AC([PSB], [VTB], out=vt[:], in_=pst[:], func=AF.Gelu)
                                DV("bn_stats", [VTB], [BSTB], out=bst[:, 0:6], in_=vt[:])
                                DV("bn_aggr", [BSTB], [BSTB], out=bst[:, 6:8], in_=bst[:, 0:6])
                                AC([BSTB, CONST], [BSTB], out=bst[:, 7:8], in_=bst[:, 7:8], func=AF.Ln, scale=1.0, bias=epsb[:, 0:1])
                                AC([BSTB], [BSTB], out=bst[:, 7:8], in_=bst[:, 7:8], func=AF.Exp, scale=-0.5)
                                DV("tensor_scalar", [VTB, BSTB], [VHB], out=vh[:], in0=vt[:], scalar1=bst[:, 6:7], scalar2=bst[:, 7:8],
                                   op0=ALU.subtract, op1=ALU.mult)
                                DV("tensor_tensor", [VHB, GBC], [VHB], out=vh[:], in0=vh[:], in1=gbc[:], op=ALU.mult)
                                DV("tensor_tensor", [VHB, BBC], [VNB], out=vn[:, b, :], in0=vh[:], in1=bbc[:], op=ALU.add)
                    def stA(hd):
                        AC([SGB, L0C], [LFB], out=lf[:], in_=sg4[:, hd, :], func=AF.Ln, scale=oml(hd), bias=lbp(hd))
                        DV("tensor_scalar", [SGB, L0C], [KKB], out=kk[:], in0=sg4[:, hd, :], scalar1=noml(hd), scalar2=oml(hd), op0=ALU.mult, op1=ALU.add)
                        DV("tensor_tensor_scan", [LFB, CMB], [GB], out=G[:], data0=cmask[:], data1=lf[:], initial=0.0, op0=ALU.mult, op1=ALU.add)
                        DV("tensor_tensor", [GB], [D1B], out=D1[:].rearrange("p (c t) -> p c t", t=64), in0=G3,
                           in1=G3[:, :, 31:32].to_broadcast([128, 8, 64]), op=ALU.subtract)
                        AC([D1B], [E1B], out=E1[:], in_=D1[:], func=AF.Exp)
                        AC([D1B], [E2B], out=E2[:], in_=D1[:], func=AF.Exp, scale=-1.0)
                        AC([GB], [EGBs[hd]], out=eg[:, hd, 0:8], in_=G3[:, :, 31], func=AF.Exp)
                        AC([GB], [EGBs[hd]], out=eg[:, hd, 8:16], in_=G3[:, :, 63], func=AF.Exp)
                        DV("tensor_tensor", [QSB, E1B], [QEBs[hd]], out=qexp[:, hd, :], in0=qs4[:, hd, :], in1=E1[:], op=ALU.mult)
                        DV("tensor_tensor", [KKB, E2B], [KEBs[hd]], out=kexp[:, hd, :], in0=kk[:], in1=E2[:], op=ALU.mult)
                        DV("tensor_tensor", [GB], [D3B], out=D3[:].rearrange("p (c t) -> p c t", t=64), in0=G3[:, :, 63:64].to_broadcast([128, 8, 64]),
                           in1=G3, op=ALU.subtract)
                        AC([D3B], [E3B], out=E3[:], in_=D3[:], func=AF.Exp)
                        DV("tensor_tensor", [KKB, E3B], [KDBs[hd]], out=kdec[:, hd, :], in0=kk[:], in1=E3[:], op=ALU.mult)

                    def stB(hd):
                        pS, PSS = psring.next()
                        for c in range(8):
                            b, par = c // 2, c % 2
                            MM([KEBs[hd], QEBs[hd]], [PSS], out=pS[par * 64:(par + 1) * 64, b * 64:(b + 1) * 64], lhsT=kexp[:, hd, c * 64:(c + 1) * 64],
                               rhs=qexp[:, hd, c * 64:(c + 1) * 64], start=True, stop=True)
                        DV("tensor_tensor", [PSS, TMB], [SMBs[hd]], out=smT[:, hd, :], in0=pS[:, 0:256], in1=tmask[:], op=ALU.mult)
                        pT_, PTB = psring.next()
                        pTb = pT_[:].bitcast(BF16)
                        for b in range(4):
                            P.I("pe", "transpose", out=pTb[:, b * 128:(b + 1) * 128], in_=kdec[:, hd, b * 128:(b + 1) * 128], identity=ident_b[:],
                                reads=[KDBs[hd], CONST], writes=[PTB])
                        AC([PTB], [KTBs[hd]], out=kdT[:, hd, :], in_=pTb[:, 0:512], func=AF.Copy)
                        DV("tensor_copy", [SCB], [SWB], out=Sw[:, 0, :], in_=Sc[:, hd, :])
                        pUs = [psring.next(), psring.next()]
                        for c in range(8):
                            b, par = c // 2, c % 2
                            pU, PUB = pUs[par]
                            MM([KTBs[hd], ITB], [PUB], out=pU[:, b * 128:(b + 1) * 128], lhsT=kdT[par * 64:(par + 1) * 64, hd, b * 128:(b + 1) * 128],
                               rhs=itok[par * 64:(par + 1) * 64, b, hd * 128:(hd + 1) * 128], start=True, stop=True)
                        for c in range(8):
                            b, par = c // 2, c % 2
                            pU, PUB = pUs[par]
                            DV("scalar_tensor_tensor", [SWB, EGBs[hd], PUB], [SWB], out=Sw[:, c + 1, :], in0=Sw[:, c, :], scalar=eg[:, hd, 8 + c:9 + c],
                               in1=pU[:, b * 128:(b + 1) * 128], op0=ALU.mult, op1=ALU.add)
                        DV("tensor_copy", [SWB], [SCB], out=Sc[:, hd, :], in_=Sw[:, 8, :])
                        Sg, SGGB = sgs[hd % 2]
                        DV("tensor_tensor", [SWB, EGBs[hd]], [SGGB], out=Sg[:], in0=Sw[:, 0:8, :], in1=eg[:, hd, 0:8].unsqueeze(2).to_broadcast([128, 8, 128]), op=ALU.mult)

                    def stC(hd):
                        Sg, SGGB = sgs[hd % 2]
                        pO, POB = psring.next()
                        for c in range(8):
                            b, par = c // 2, c % 2
                            MM([SGGB, QEBs[hd]], [POB], out=pO[:, c * 64:(c + 1) * 64], lhsT=Sg[:, c, :], rhs=qexp[:, hd, c * 64:(c + 1) * 64], start=True, stop=False)
                            MM([ITB, SMBs[hd]], [POB], out=pO[:, c * 64:(c + 1) * 64], lhsT=itok[par * 64:(par + 1) * 64, b, hd * 128:(hd + 1) * 128],
                               rhs=smT[par * 64:(par + 1) * 64, hd, b * 64:(b + 1) * 64], start=False, stop=True)
                        sqs, SQSB = sqsring.next()
                        AC([POB], [SQSB], out=sqs[:, 0, :], in_=pO[:], func=AF.Square)
                        pN, PNB = psring.next()
                        MM([SQSB, CONST], [PNB], out=pN[:], lhsT=ones_b[:], rhs=sqs[:, 0, :], start=True, stop=True)
                        rt, RB = rstd_from_psum(pN, PNB, 128)
                        DV("scalar_tensor_tensor", [POB, RB, L0C], [T1B], out=t1[:], in0=pO[:], scalar=an(hd), in1=rt[:], op0=ALU.mult, op1=ALU.mult)
                        DV("tensor_tensor", [T1B, GTB], [MXBs[hd]], out=mixT[:, hd, :], in0=t1[:], in1=gate4[:, hd, :], op=ALU.mult)

                    L0ORD = {0: "G0 G1 G2 G3 G4 G5 A0 A1 B0 A2 B1 C0 A3 B2 C1 B3 C2 C3",
                             1: "G0 G1 G5 A0 G2 A1 B0 G3 A2 B1 C0 G4 A3 B2 C1 B3 C2 C3"}[cfg.get("l0ord", 0)]
                    for step in [(w[0], int(w[1])) for w in L0ORD.split()]:
                        {"A": stA, "B": stB, "C": stC, "G": proj_group}[step[0]](step[1])
                    for g in range(4):
                        pM, PMB = psring.next()
                        for b in range(4):
                            MM([VNB, WST], [PMB], out=pM[:, b * 128:(b + 1) * 128], lhsT=vn[:, b, g * 128:(g + 1) * 128], rhs=wsT[:, g, :], start=True, stop=True)
                        DV("tensor_tensor", [PMB, BIB], [T1B], out=t1[:].rearrange("p (b t) -> p b t", t=128), in0=pM[:].rearrange("p (b t) -> p b t", t=128),
                           in1=biasbc[:, g * 128:(g + 1) * 128].unsqueeze(1).to_broadcast([128, 4, 128]), op=ALU.add)
                        DV("tensor_tensor", [T1B, UTB], [MXBs[4 + g]], out=mixT[:, 4 + g, :], in0=t1[:], in1=uT[:, g, :], op=ALU.mult)
                    outproj(w_out_v, wring, mixT, MXBs, msb, MSB, tm2s, None, 0, t0, alias=ALIAS)
                P.barrier()

        def outproj(w_out_v, wring, mixT, MXB, msb, MSB, tm2, TM2B, layer, t0, moff=0, alias=()):
            ti = t0 // 512
            sp_, SPB = ps_t[6], PS[6]
            for g in range(2):
                w, WB = wring.next()
                load_w(w[:], w_out_v[:, :, g * 512:(g + 1) * 512], WB)
                for j in range(4):
                    oc = g * 4 + j
                    pst, PSB = psring.next()
                    for kc in range(8):
                        P.I("pe", "matmul", reads=[WB] + (MXB if isinstance(MXB, list) else [MXB]), writes=[PSB], out=pst[:], lhsT=w[:, kc, j * 128:(j + 1) * 128],
                            rhs=mixT[:, kc, moff:moff + 512], start=(kc == 0), stop=(kc == 7))
                    sqs, SQSB = sqsring.next()
                    P.I("act", "activation", reads=[PSB], writes=[SQSB], out=sqs[:, 0, :], in_=pst[:], func=AF.Square)
                    P.I("dve", "tensor_copy", reads=[PSB], writes=[MSB] + list(alias), out=msb[:, oc, :], in_=pst[:])
                    P.I("pe", "matmul", reads=[SQSB, CONST], writes=[SPB], out=sp_[:], lhsT=ones_b[:], rhs=sqs[:, 0, :], start=(oc == 0), stop=(oc == 7))
            rt, RB = rstd_from_psum(sp_, SPB, D)
            tms = tm2 if isinstance(tm2, list) else [(tm2, TM2B)]
            for c in range(8):
                tm_, TMB_ = tms[c % len(tms)]
                P.I("dve", "scalar_tensor_tensor", reads=[MSB, RB, CONST] + list(alias), writes=[TMB_], out=tm_[:], in0=msb[:, c, :],
                    scalar=gains[:, GN["norm_mix_post"], layer, c:c + 1], in1=rt[:], op0=ALU.mult, op1=ALU.mult)
                P.I("dve", "tensor_tensor", reads=[TMB_, XT[c][ti]], writes=[XT[c][ti]], out=xT[:, c, t0:t0 + 512], in0=xT[:, c, t0:t0 + 512],
                    in1=tm_[:], op=ALU.add)

        def l1mix(sq_i):
            r0 = sq_i * S
            DV = lambda name, reads, writes, **kw: P.I("dve", name, reads=reads, writes=writes, **kw)
            AC = lambda reads, writes, **kw: P.I("act", "activation", reads=reads, writes=writes, **kw)
            MM = lambda reads, writes, **kw: P.I("pe", "matmul", reads=reads, writes=writes, **kw)
            with ExitStack() as po:
                def eno(n, s, dt=F32):
                    return po.enter_context(nc.sbuf_tensor("%s_s%d" % (n, sq_i), s, dt)), Buf(n)
                hT, HB = eno("hT1", [128, 8, S], BF16)
                oT, OTB = eno("oT1", [128, 8, S], BF16)
                OTBs = [Buf() for _ in range(8)]
                with ExitStack() as ph:
                    sq = ph.enter_context(nc.sbuf_tensor("l1sq_s%d" % sq_i, [128, 8, 512], BF16))
                    SQB = Buf()
                    for ti in range(4):
                        prenorm(ti * 512, GN["norm_mix_pre"], 1, hT, HB, ti * 512, sqbuf=(sq, SQB))
                P.barrier()
                with ExitStack() as ph:
                    def en(n, s, dt=F32):
                        return ph.enter_context(nc.sbuf_tensor("%s_s%d" % (n, sq_i), s, dt)), Buf(n)
                    qkT, QKB = en("qkT", [128, 6, S], BF16)
                    P.I("pool", "memset", ap=qkT[64:128, 2:4, :], constant=0.0, writes=[QKB])
                    P.I("pool", "memset", ap=qkT[0:64, 4:6, :], constant=0.0, writes=[QKB])
                    Vt, VB = en("Vt", [128, 16, 4, 128], BF16)
                    wraw, WQB = en("wraw", [128, 8 * 768], BF16)
                    wqk = wraw[:].rearrange("p (k n) -> p k n", n=768)
                    qtbs = [en("qtb%d" % i, [128, 512], BF16) for i in range(2)]
                    qtring = Ring(qtbs)
                    pTs = [(wraw[:, i * 512:(i + 1) * 512], Buf("pt%d" % i)) for i in range(4)] + \
                          [(wraw[:, 4096 + i * 512:4096 + (i + 1) * 512], Buf("pt%d" % (4 + i))) for i in range(3)]
                    ptring = Ring(pTs)
                    evs = [(wraw[:, 2048 + i * 1024:2048 + (i + 1) * 1024].bitcast(F32), Buf("ev%d" % i)) for i in range(2)]
                    evring = Ring(evs)
                    ALIASW = [b for _, b in pTs] + [b for _, b in evs]
                    swapm, SWB_ = en("swapm", [128, 128])
                    am, AMB = en("am", [128, 9, 512], BF16)
                    cs, CSB = en("cs", [128, 16, 8])
                    sn, SNB = en("sn", [128, 16, 8])
                    rtmp, RTB = en("rtmp", [128, 4, 8, 8])
                    RTBs = [Buf() for _ in range(4)]
                    load_w(am[:], am_d[:, :, :], AMB)
                    P.I("pool", "memset", ap=Vt[:, :, 0::2, 64:128], constant=1.0, writes=[VB])
                    P.I("pool", "memset", ap=Vt[:, :, 1::2, 0:64], constant=1.0, writes=[VB])
                    P.I("pool", "tensor_copy", out=swapm[:, 0:64], in_=ident_f[:, 64:128], reads=[CONST], writes=[SWB_])
                    P.I("pool", "tensor_copy", out=swapm[:, 64:128], in_=ident_f[:, 0:64], reads=[CONST], writes=[SWB_])
                    ki, KIB = en("ki", [128, 128], I32)
                    kf, KFB = en("kf", [128, 128])
                    posr, PRB = ki[0:16, :], KIB
                    posf, PFB = kf[0:16, :], KFB
                    posT, PTB_ = en("posT", [128, 16])
                    invf, IVB = en("invf", [128, 8])
                    ang, ANB = en("ang", [128, 128])
                    a2, A2B = en("a2", [128, 128])
                    mk_, MKB = kf, KFB
                    P.dma(invf[:], cst_d[:, 896:904], writes=[IVB])
                    P.dma(posr[:], pos_d[r0:r0 + S].rearrange("(b p) -> b p", p=128), writes=[PRB])
                    DV("tensor_copy", [PRB], [PFB], out=posf[:], in_=posr[:])
                    pst, PSB = psring.next()
                    P.I("pe", "transpose", out=pst[:, 0:16], in_=posf[:, :], identity=ident_f[0:16, 0:16], reads=[PFB, CONST], writes=[PSB])
                    DV("tensor_copy", [PSB], [PTB_], out=posT[:], in_=pst[:, 0:16])
                    DV("tensor_tensor", [PTB_, IVB], [ANB], out=ang[:].rearrange("p (b j) -> p b j", j=8), in0=posT[:].unsqueeze(2).to_broadcast([128, 16, 8]),
                       in1=invf[:].unsqueeze(1).to_broadcast([128, 16, 8]), op=ALU.mult)

                    def sin_of(shift, dst, DB):
                        DV("tensor_scalar", [ANB], [A2B], out=a2[:], in0=ang[:], scalar1=float(shift), scalar2=None, op0=ALU.add)
                        DV("tensor_scalar", [A2B], [KFB], out=kf[:], in0=a2[:], scalar1=1.0 / TWO_PI, scalar2=None, op0=ALU.mult)
                        DV("tensor_copy", [KFB], [KIB], out=ki[:], in_=kf[:])
                        DV("tensor_copy", [KIB], [KFB], out=kf[:], in_=ki[:])
                        DV("scalar_tensor_tensor", [KFB, A2B], [A2B], out=a2[:], in0=kf[:], scalar=-TWO_PI, in1=a2[:], op0=ALU.mult, op1=ALU.add)
                        DV("tensor_scalar", [A2B], [MKB], out=mk_[:], in0=a2[:], scalar1=math.pi, scalar2=None, op0=ALU.is_gt)
                        DV("scalar_tensor_tensor", [MKB, A2B], [A2B], out=a2[:], in0=mk_[:], scalar=-TWO_PI, in1=a2[:], op0=ALU.mult, op1=ALU.add)
                        DV("tensor_scalar", [A2B], [MKB], out=mk_[:], in0=a2[:], scalar1=-math.pi, scalar2=None, op0=ALU.is_lt)
                        DV("scalar_tensor_tensor", [MKB, A2B], [A2B], out=a2[:], in0=mk_[:], scalar=TWO_PI, in1=a2[:], op0=ALU.mult, op1=ALU.add)
                        DV("tensor_scalar", [A2B], [A2B], out=a2[:], in0=a2[:], scalar1=math.pi, scalar2=-math.pi, op0=ALU.min, op1=ALU.max)
                        AC([A2B], [DB], out=dst[:].rearrange("p b j -> p (b j)"), in_=a2[:], func=AF.Sin)
                    sin_of(0.0, sn, SNB)
                    sin_of(math.pi / 2, cs, CSB)

                    w_v = w_in_odd.rearrange("(kc p) n -> p kc n", p=128)
                    poring = Ring([(ps_t[i], PS[i]) for i in (4, 5, 6, 7)])
                    for qt in range(4):
                        for part in range(3):
                            P.dma(wqk[:, :, part * 256:(part + 1) * 256], w_v[:, :, part * 1024 + qt * 256: part * 1024 + (qt + 1) * 256],
                                  writes=[WQB] + ALIASW, eng=wdma_eng)
                        def proj_a(tb):
                            pA, PAB = psring.next()
                            for kc in range(8):
                                MM([WQB, HB], [PAB], out=pA[:], lhsT=hT[:, kc, tb * 128:(tb + 1) * 128], rhs=wqk[:, kc, 0:512], start=(kc == 0), stop=(kc == 7))
                            pB, PBB = psring.next()
                            for kc in range(8):
                                MM([WQB, HB], [PBB], out=pB[:, 0:256], lhsT=hT[:, kc, tb * 128:(tb + 1) * 128], rhs=wqk[:, kc, 512:768], start=(kc == 0), stop=(kc == 7))
                            qtb, QTB = qtring.next()
                            AC([PAB], [QTB], out=qtb[:], in_=pA[:], func=AF.Copy)
                            pA3 = pA[:].rearrange("p (h d) -> p h d", d=64)
                            q3 = qtb[:].rearrange("p (h d) -> p h d", d=64)
                            cb = cs[:, tb, :].unsqueeze(1).to_broadcast([128, 8, 8])
                            sb_ = sn[:, tb, :].unsqueeze(1).to_broadcast([128, 8, 8])
                            DV("tensor_tensor", [PAB, CSB], [RTBs[0]], out=rtmp[:, 0, :, :], in0=pA3[:, :, 0:8], in1=cb, op=ALU.mult)
                            DV("tensor_tensor", [PAB, SNB], [RTBs[1]], out=rtmp[:, 1, :, :], in0=pA3[:, :, 8:16], in1=sb_, op=ALU.mult)
                            DV("tensor_tensor", [PAB, SNB], [RTBs[2]], out=rtmp[:, 2, :, :], in0=pA3[:, :, 0:8], in1=sb_, op=ALU.mult)
                            DV("tensor_tensor", [PAB, CSB], [RTBs[3]], out=rtmp[:, 3, :, :], in0=pA3[:, :, 8:16], in1=cb, op=ALU.mult)
                            DV("tensor_tensor", [RTBs[0], RTBs[1], QTB], [QTB], out=q3[:, :, 0:8], in0=rtmp[:, 0, :, :], in1=rtmp[:, 1, :, :], op=ALU.subtract)
                            DV("tensor_tensor", [RTBs[2], RTBs[3], QTB], [QTB], out=q3[:, :, 8:16], in0=rtmp[:, 2, :, :], in1=rtmp[:, 3, :, :], op=ALU.add)
                            pB3 = pB[:, 0:256].rearrange("p (h d) -> p h d", d=64)
                            AC([PBB], [VB], out=Vt[:, tb, 0::2, 0:64], in_=pB3[:, 0::2, :], func=AF.Copy)
                            AC([PBB], [VB], out=Vt[:, tb, 1::2, 64:128], in_=pB3[:, 1::2, :], func=AF.Copy)
                            return qtb, QTB

                        def proj_b(tb, qtb, QTB):
                            pT_, PTB = psring.next()
                            pTb = pT_[:].bitcast(BF16)
                            for j in range(4):
                                P.I("pe", "transpose", out=pTb[:, j * 128:(j + 1) * 128], in_=qtb[:, j * 128:(j + 1) * 128], identity=ident_b[:],
                                    reads=[QTB, CONST], writes=[PTB])
                            DV("tensor_copy", [PTB], [QKB], out=qkT[:, 0:2, tb * 128:(tb + 1) * 128], in_=pTb[:, 0:256].rearrange("p (j t) -> p j t", t=128))
                            DV("tensor_copy", [PTB], [QKB], out=qkT[0:64, 2:4, tb * 128:(tb + 1) * 128], in_=pTb[0:64, 256:512].rearrange("p (j t) -> p j t", t=128))
                            DV("tensor_copy", [PTB], [QKB], out=qkT[64:128, 4:6, tb * 128:(tb + 1) * 128], in_=pTb[64:128, 256:512].rearrange("p (j t) -> p j t", t=128))

                        prev = None
                        for tb in range(16):
                            cur = proj_a(tb)
                            if prev is not None:
                                proj_b(*prev)
                            prev = (tb,) + cur
                        proj_b(*prev)
                        nh = 4 if cfg.get("l1stage", 9) > 1 else 0
                        items = [(2 * pr + e, T, kb) for pr in range(nh // 2) for T in range(4) for kb in range(4 * T + 4) for e in range(2)]
                        PRE = 3
                        qk_state, pO_of, deferred = {}, {}, []
                        qring = Ring([(ps_t[i], PS[i]) for i in range(4)])

                        def emit_qk(i):
                            lh, T, kb = items[i]
                            j, po_ = lh // 2, (lh % 2) * 64
                            col0 = max(0, kb - 4 * T) * 128
                            pS, PSS = qring.next()
                            MM([QKB], [PSS], out=pS[:, col0:512], lhsT=qkT[:, (2 if po_ == 0 else 4) + j, kb * 128:(kb + 1) * 128],
                               rhs=qkT[:, j, T * 512 + col0:(T + 1) * 512], start=True, stop=True)
                            qk_state[i] = (pS, PSS, col0)

                        def emit_rest(i):
                            lh, T, kb = items[i]
                            hglob = qt * 4 + lh
                            nkb = 4 * T + 4
                            rel = 4 * T - kb
                            midx = rel + 3 if rel <= 4 else 8
                            if kb == 0:
                                pO_of[(lh, T)] = poring.next()
                            pO, POB = pO_of[(lh, T)]
                            pS, PSS, col0 = qk_state.pop(i)
                            pt, PTT = ptring.next()
                            AC([PSS], [PTT, WQB], out=pt[:, col0:512], in_=pS[:, col0:512], func=AF.Exp, scale=0.125)
                            DV("tensor_tensor", [PTT, AMB], [PTT], out=pt[:, col0:512], in0=pt[:, col0:512], in1=am[:, midx, col0:512], op=ALU.mult)
                            MM([VB, PTT], [POB], out=pO[:, col0:512], lhsT=Vt[:, kb, lh, :], rhs=pt[:, col0:512], start=(kb == 0), stop=(kb == nkb - 1))
                            if kb == nkb - 1 and lh % 2 == 1:
                                pOA, POA = pO_of[(lh - 1, T)]
                                pOB_, POBB = pO, POB
                                num, NUMB = evring.next()
                                zz, ZB = evring.next()

                                def ep1(pOA=pOA, POA=POA, pOB_=pOB_, POBB=POBB, num=num, NUMB=NUMB, zz=zz, ZB=ZB):
                                    AC([POA], [NUMB, WQB], out=num[0:64, :], in_=pOA[0:64, :], func=AF.Copy)
                                    AC([POA], [ZB, WQB], out=zz[64:128, :], in_=pOA[64:128, :], func=AF.Ln)
                                    AC([POBB], [NUMB], out=num[64:128, :], in_=pOB_[64:128, :], func=AF.Copy)
                                    AC([POBB], [ZB], out=zz[0:64, :], in_=pOB_[0:64, :], func=AF.Ln)
                                    AC([ZB], [ZB], out=zz[:, :], in_=zz[:, :], func=AF.Exp, scale=-1.0)

                                def ep2(num=num, NUMB=NUMB, zz=zz, ZB=ZB, hglob=hglob, T=T, pOA=pOA, POA=POA):
                                    pSw, PSWB = pOA, POA
                                    MM([ZB, SWB_], [PSWB], out=pSw[:], lhsT=swapm[:], rhs=zz[:, :], start=True, stop=True)
                                    DV("tensor_tensor", [NUMB, PSWB], [OTBs[hglob // 2]], out=oT[:, hglob // 2, T * 512:(T + 1) * 512],
                                       in0=num[:, :], in1=pSw[:], op=ALU.mult)
                                deferred.append((i + 1, ep1))
                                deferred.append((i + 4, ep2))

                        for i in range(len(items) + PRE + 8):
                            if i < len(items):
                                emit_qk(i)
                            jx = i - PRE
                            if 0 <= jx < len(items):
                                emit_rest(jx)
                            due = [d for d in deferred if d[0] <= jx]
                            for d in due:
                                deferred.remove(d)
                                d[1]()
                        assert not deferred
                P.barrier()
                with ExitStack() as ph:
                    def en(n, s, dt=F32):
                        return ph.enter_context(nc.sbuf_tensor("%s_s%d" % (n, sq_i), s, dt)), Buf(n)
                    wcs = [en("wo%d" % i, [128, 8, 512], BF16) for i in range(2)]
                    wring = Ring(wcs)
                    msb, MSB = en("msb1", [128, 8, 512])
                    tm2s = [en("tm21", [128, 512]), en("tm22", [128, 512])]
                    w_out_v = w_out_odd.rearrange("(kc p) n -> p kc n", p=128)
                    for ti in range(4):
                        outproj(w_out_v, wring, oT, OTBs, msb, MSB, tm2s, None, 1, ti * 512, moff=ti * 512)
            P.barrier()

        for sq_i in range(nseq):
            r0 = sq_i * S
            with ExitStack() as ph:
                xin = [ph.enter_context(nc.sbuf_tensor("xin%d_%d" % (sq_i, i), [128, D], F32)) for i in range(3)]
                xring = Ring([(xin[i], Buf()) for i in range(3)])
                for tb in range(16):
                    xi, XIB = xring.next()
                    P.dma(xi[:], x_d[r0 + tb * 128: r0 + (tb + 1) * 128, :], writes=[XIB])
                    for half in range(2):
                        pst, PSB = psring.next()
                        for j in range(4):
                            c = half * 4 + j
                            P.op("pe", lambda e, c=c, j=j, pst=pst, xi=xi: e.transpose(
                                out=pst[:, j * 128:(j + 1) * 128], in_=xi[:, c * 128:(c + 1) * 128], identity=ident_f[:]),
                                reads=[XIB, CONST], writes=[PSB])
                        eng = "act" if half == 0 else "dve"
                        dst = xT[:, half * 4:(half + 1) * 4, tb * 128:(tb + 1) * 128]
                        src = pst[:].rearrange("p (j t) -> p j t", t=128)
                        wr = [XT[half * 4 + j][tb // 4] for j in range(4)]
                        if eng == "act":
                            P.op("act", lambda e, dst=dst, src=src: e.activation(out=dst, in_=src, func=AF.Copy), reads=[PSB], writes=wr)
                        else:
                            P.op("dve", lambda e, dst=dst, src=src: e.tensor_copy(out=dst, in_=src), reads=[PSB], writes=wr)

            P.barrier()
            def run_ffn(layer):
                with ExitStack() as ph:
                    en = lambda n, s, dt=F32: ph.enter_context(nc.sbuf_tensor("%s_s%d_l%d" % (n, sq_i, layer), s, dt))
                    h2 = en("h2", [128, 8, 1024], BF16)
                    hid = en("hid", [128, 32, 1024], BF16)
                    w1 = [en("w1_%d" % i, [128, 8, 512], BF16) for i in range(2)]
                    w2 = [en("w2_%d" % i, [128, 32, 128], BF16) for i in range(2)]
                    rl = [en("rl%d" % i, [128, 512], F32) for i in range(3)]
                    fsq = en("fsq", [128, 8, 512], BF16)
                    pools = (h2, [Buf(), Buf()], hid, [Buf() for _ in range(32)],
                             Ring([(w1[i], Buf()) for i in range(2)]), Ring([(w2[i], Buf()) for i in range(2)]),
                             (fsq, Buf()), None, Ring([(rl[i], Buf()) for i in range(3)]))
                    ffn(layer, pools)
                P.barrier()

            if do_l0mix:
                l0mix(sq_i)
            if do_l0ffn:
                run_ffn(0)
            if do_l1mix:
                l1mix(sq_i)
            if do_l1ffn:
                run_ffn(1)

            with ExitStack() as ph:
                xo = [ph.enter_context(nc.sbuf_tensor("xo%d_%d" % (sq_i, i), [128, D], F32)) for i in range(3)]
                oring = Ring([(xo[i], Buf()) for i in range(3)])
                for tb in range(16):
                    xo_, XOB = oring.next()
                    for half in range(2):
                        pst, PSB = psring.next()
                        for j in range(4):
                            c = half * 4 + j
                            P.I("pe", "transpose", out=pst[:, j * 128:(j + 1) * 128], in_=xT[:, c, tb * 128:(tb + 1) * 128],
                                identity=ident_f[:], reads=[XT[c][tb // 4], CONST], writes=[PSB])
                        dst = xo_[:, half * 512:(half + 1) * 512]
                        if half == 0:
                            P.op("act", lambda e, dst=dst, pst=pst: e.activation(out=dst, in_=pst[:], func=AF.Copy), reads=[PSB], writes=[XOB])
                        else:
                            P.op("dve", lambda e, dst=dst, pst=pst: e.tensor_copy(out=dst, in_=pst[:]), reads=[PSB], writes=[XOB])
                    P.dma(out_d[r0 + tb * 128: r0 + (tb + 1) * 128, :], xo_[:], reads=[XOB], is_output=True)
            P.barrier()

        P.emit()
    return nc


_INPUT_ORDER = ["x", "positions", "norm_mix_pre", "norm_mix_post", "norm_ffn_pre", "norm_ffn_post",
                "w_in_even", "lb_table", "a_norm", "b_ln_g", "b_ln_b", "b_ws", "b_bias", "w_out_even",
                "w_in_odd", "w_out_odd", "w_ff1", "w_ff2"]


def host_tables():
    cst = np.zeros((128, 904), np.float32)
    cst[:, 0:128] = np.eye(128, dtype=np.float32)
    cm = np.ones((128, 512), np.float32)
    cm[:, ::64] = 0.0
    cst[:, 128:640] = cm
    p = np.arange(128)[:, None] % 64
    t = np.arange(256)[None, :] % 64
    cst[:, 640:896] = (p <= t).astype(np.float32)
    half = 8
    cst[:, 896:904] = (ROPE_THETA ** (-np.arange(half, dtype=np.float32) / half)).astype(np.float32)[None, :]
    am = np.zeros((128, 9, 512), np.float32)
    i = np.arange(128)[:, None]
    jj = np.arange(512)[None, :]
    for m in range(9):
        rel = m - 3 if m < 8 else 5
        dl = 128 * rel + jj - i
        c = ((dl >= 0) & (dl <= 128)).astype(np.float32) + ((dl >= 0) & (dl <= 512) & (dl % 4 == 0)).astype(np.float32) \
            + ((dl >= 0) & (dl <= 2048) & (dl % 16 == 0)).astype(np.float32)
        am[:, m, :] = c
    return cst, am


def make_in_maps(inputs, n_cores=NC8, nseq=2):
    f = lambda a: np.ascontiguousarray(np.asarray(a))
    x = f(inputs["x"]).astype(np.float32, copy=False)
    pos = f(inputs["positions"]).astype(np.int32, copy=False)
    shared = {
        "norm_mix_pre": f(inputs["norm_mix_pre"]), "norm_mix_post": f(inputs["norm_mix_post"]),
        "norm_ffn_pre": f(inputs["norm_ffn_pre"]), "norm_ffn_post": f(inputs["norm_ffn_post"]),
        "w_in_even": f(inputs["w_in_even"]).reshape(D, 3072), "lb_table": f(inputs["lb_table"]),
        "a_norm": f(inputs["a_norm"]).reshape(512), "b_ln_g": f(inputs["b_ln_g"]).reshape(1, 512),
        "b_ln_b": f(inputs["b_ln_b"]).reshape(1, 512), "b_ws": f(inputs["b_ws"]).reshape(4, 128, 128),
        "b_bias": f(inputs["b_bias"]).reshape(1, 512), "w_out_even": f(inputs["w_out_even"]).reshape(D, D),
        "w_in_odd": f(inputs["w_in_odd"]).reshape(D, 3072), "w_out_odd": f(inputs["w_out_odd"]).reshape(D, D),
        "w_ff1": f(inputs["w_ff1"]), "w_ff2": f(inputs["w_ff2"]),
    }
    shared["cst"], shared["amask"] = host_tables()
    maps = []
    for c in range(n_cores):
        m = dict(shared)
        m["x"] = x[c * nseq:(c + 1) * nseq].reshape(nseq * S, D)
        m["positions"] = pos[c * nseq:(c + 1) * nseq].reshape(nseq * S)
        maps.append(m)
    return maps


def kernel(**inputs):
    nc = build(nseq=2)
    maps = make_in_maps(inputs)
    res = run_bass_kernel_spmd(nc, maps, core_ids=list(range(NC8)))
    outs = [np.asarray(r["out"]).reshape(2, S, D) for r in res.results]
    return np.concatenate(outs, axis=0).astype(np.float32, copy=False)
```

```python
import math
import numpy as np
from contextlib import ExitStack
import concourse.bass as bass
import concourse.mybir as mybir
from concourse.bass_utils import run_bass_kernel_spmd

F32 = mybir.dt.float32
BF16 = mybir.dt.bfloat16
I32 = mybir.dt.int32
ALU = mybir.AluOpType
AF = mybir.ActivationFunctionType

N_DMA_SEMS = 24
STRICT_SAME_ENGINE = False
D = 1024
S = 2048
NC8 = 8
EPS = 1e-6
DFF = 4096
ROPE_THETA = 500000.0


class Ins:
    __slots__ = ("eng", "fn", "deps", "sig", "seq", "is_dma", "dsem", "dval", "idx")

    def __init__(self, eng, fn, is_dma=False):
        self.eng = eng
        self.fn = fn
        self.deps = []
        self.sig = False
        self.seq = 0
        self.is_dma = is_dma
        self.dsem = None
        self.dval = 0
        self.idx = 0


class Buf:
    __slots__ = ("name", "last_w", "readers", "excl")

    def __init__(self, name="", excl=False):
        self.name = name
        self.last_w = None
        self.readers = {}
        self.excl = excl


class Prog:
    def __init__(self, nc):
        self.nc = nc
        self.streams = {"pe": [], "act": [], "dve": [], "pool": [], "sp": []}
        self.n_dma = 0
        self.dma_list = []
        self.out_dmas = []

    def _dep(self, ins, prod, kind):
        if prod is None or prod is ins:
            return
        if not prod.is_dma and not ins.is_dma and prod.eng == ins.eng:
            if ins.eng == "pe" or (kind != "RAW" and not STRICT_SAME_ENGINE):
                return
        if prod not in ins.deps:
            ins.deps.append(prod)
            prod.sig = True

    def op(self, eng, fn, reads=(), writes=(), is_dma=False):
        ins = Ins(eng, fn, is_dma)
        ex = [b for b in reads if b.excl]
        if ex:
            reads = [b for b in reads if not b.excl]
            writes = list(writes) + [b for b in ex if b not in writes]
        for b in reads:
            self._dep(ins, b.last_w, "RAW")
        for b in writes:
            self._dep(ins, b.last_w, "WAW")
            for r in b.readers.values():
                self._dep(ins, r, "WAR")
        for b in reads:
            b.readers[("dma", id(ins)) if is_dma else eng] = ins
        for b in writes:
            b.last_w = ins
            b.readers = {}
        self.streams[eng].append(ins)
        if is_dma:
            ins.idx = self.n_dma
            self.n_dma += 1
            self.dma_list.append(ins)
            ins.sig = True
        return ins

    def I(self, eng, name, reads=(), writes=(), **kw):
        return self.op(eng, lambda e: getattr(e, name)(**kw), reads, writes)

    def dma(self, out, in_, reads=(), writes=(), eng="sp", is_output=False, **kw):
        ins = self.op(eng, lambda e: e.dma_start(out=out, in_=in_, **kw), reads, writes, is_dma=True)
        if is_output:
            self.out_dmas.append(ins)
        return ins

    def barrier(self):
        drains = {}
        for e in self.streams:
            ins = Ins(e, lambda eng: eng.drain())
            ins.sig = True
            self.streams[e].append(ins)
            drains[e] = ins
        dmas = list(self.dma_list[getattr(self, "_bar_dma", 0):])
        self._bar_dma = len(self.dma_list)
        for e in self.streams:
            w = Ins(e, None)
            w.deps = [d for k, d in drains.items() if k != e] + dmas
            self.streams[e].append(w)

    def emit(self):
        nc = self.nc
        with ExitStack() as st:
            esem = {e: st.enter_context(nc.semaphore("s_" + e)) for e in self.streams}
            dsems = [st.enter_context(nc.semaphore("d%d" % i)) for i in range(N_DMA_SEMS)]
            dcount = [0] * N_DMA_SEMS
            for e, lst in self.streams.items():
                c = 0
                for ins in lst:
                    if not ins.is_dma and ins.sig:
                        c += 1
                        ins.seq = c
            prev_on_sem = {}
            half = N_DMA_SEMS // 2
            cnt = {"sp": 0, "pool": 0}
            for ins in self.dma_list:
                q = "pool" if ins.eng == "pool" else "sp"
                j = (cnt[q] % half) + (half if q == "pool" else 0)
                cnt[q] += 1
                dcount[j] += 16
                ins.dsem = j
                ins.dval = dcount[j]
                p = prev_on_sem.get(j)
                if p is not None and p not in ins.deps:
                    ins.deps.append(p)
                prev_on_sem[j] = ins
            fin = Ins("sp", None)
            fin.deps = list(self.out_dmas)
            self.streams["sp"].append(fin)
            blk = st.enter_context(nc.Block())

            def mk(ename):
                lst = self.streams[ename]

                def body(eng):
                    known = {}
                    for ins in lst:
                        for d in ins.deps:
                            if d.is_dma:
                                key, val, sem = ("d", d.dsem), d.dval, dsems[d.dsem]
                            else:
                                key, val, sem = ("e", d.eng), d.seq, esem[d.eng]
                            if known.get(key, 0) >= val:
                                continue
                            eng.wait_ge(sem, val)
                            known[key] = val
                        if ins.fn is None:
                            continue
                        r = ins.fn(eng)
                        if ins.is_dma:
                            r.then_inc(dsems[ins.dsem], 16)
                        elif ins.sig:
                            r.then_inc(esem[ename], 1)
                return body

            blk.tensor(mk("pe"))
            blk.scalar(mk("act"))
            blk.vector(mk("dve"))
            blk.gpsimd(mk("pool"))
            blk.sync(mk("sp"))


class Ring:
    def __init__(self, items):
        self.items = items
        self.i = 0

    def next(self):
        it = self.items[self.i % len(self.items)]
        self.i += 1
        return it


def build(nseq=2, cfg=None):
    cfg = cfg or {}
    do_l0mix = cfg.get("l0mix", True)
    do_l0ffn = cfg.get("l0ffn", True)
    do_l1mix = cfg.get("l1mix", True)
    do_l1ffn = cfg.get("l1ffn", True)

    nc = bass.Bass("TRN2", target_bir_lowering=False)
    dr = lambda n, s, dt=F32, kind="ExternalInput": nc.dram_tensor(n, s, dt, kind=kind).ap()
    x_d = dr("x", [nseq * S, D])
    pos_d = dr("positions", [nseq * S], I32)
    g_d = {k: dr(k, [2, D]) for k in ("norm_mix_pre", "norm_mix_post", "norm_ffn_pre", "norm_ffn_post")}
    w_in_even = dr("w_in_even", [D, 3072])
    lb_table = dr("lb_table", [3, 512])
    a_norm = dr("a_norm", [512])
    b_ln_g = dr("b_ln_g", [1, 512])
    b_ln_b = dr("b_ln_b", [1, 512])
    b_ws = dr("b_ws", [4, 128, 128])
    b_bias = dr("b_bias", [1, 512])
    w_out_even = dr("w_out_even", [D, D])
    w_in_odd = dr("w_in_odd", [D, 3072])
    w_out_odd = dr("w_out_odd", [D, D])
    w_ff1 = dr("w_ff1", [2, D, DFF])
    w_ff2 = dr("w_ff2", [2, DFF, D])
    out_d = dr("out", [nseq * S, D], kind="ExternalOutput")

    P = Prog(nc)
    st = ExitStack()
    with st:
        def sb(name, shape, dt=F32):
            return st.enter_context(nc.sbuf_tensor(name, shape, dt))

        xT = sb("xT", [128, 8, S])
        XT = [[Buf() for _ in range(4)] for _ in range(8)]
        cst_d = dr("cst", [128, 904])
        am_d = dr("amask", [128, 9, 512])
        ident_f = sb("ident_f", [128, 128])
        ident_b = sb("ident_b", [128, 128], BF16)
        ones_b = sb("ones_b", [128, 128], BF16)
        ones_f = sb("ones_f", [128, 128])
        gains = sb("gains", [128, 4, 2, 8])
        epsb = sb("epsb", [128, 1])
        CONST = Buf("const")
        GN = {"norm_mix_pre": 0, "norm_mix_post": 1, "norm_ffn_pre": 2, "norm_ffn_post": 3}

        ps_t = [st.enter_context(nc.psum_tensor("ps%d" % i, [128, 512], F32)) for i in range(8)]
        PS = [Buf("ps%d" % i, excl=True) for i in range(8)]
        psring = Ring([(ps_t[i], PS[i]) for i in range(6)])

        P.op("pool", lambda e: e.memset(ones_b[:], 1.0), writes=[CONST])
        P.op("pool", lambda e: e.memset(ones_f[:], 1.0), writes=[CONST])
        P.op("pool", lambda e: e.memset(epsb[:], EPS), writes=[CONST])
        P.dma(ident_f[:], cst_d[:, 0:128], writes=[CONST])
        P.op("pool", lambda e: e.tensor_copy(out=ident_b[:], in_=ident_f[:]), reads=[CONST], writes=[CONST])
        gstage = sb("gstage", [64, 128])
        GSB = Buf()
        for k, gi in GN.items():
            P.dma(gstage[gi * 16:(gi + 1) * 16, :], g_d[k].rearrange("l (c p) -> (l c) p", p=128), writes=[GSB])
        P.op("pe", lambda e: e.transpose(out=ps_t[0][:, 0:64], in_=gstage[:, :], identity=ident_f[0:64, 0:64]),
             reads=[GSB, CONST], writes=[PS[0]])
        P.op("dve", lambda e: e.tensor_copy(out=gains[:].rearrange("p a b c -> p (a b c)"), in_=ps_t[0][:, 0:64]),
             reads=[PS[0]], writes=[CONST])

        wdma_eng = "pool"

        def load_w(dst_ap, src_ap, buf):
            return P.dma(dst_ap, src_ap, writes=[buf], eng=wdma_eng)

        sqs_t = [sb("sqs%d" % i, [128, 1, 512], BF16) for i in range(3)]
        sqsring = Ring([(sqs_t[i], Buf()) for i in range(3)])
        rstd_t = [sb("rstd%d" % i, [128, 512]) for i in range(3)]
        rsring = Ring([(rstd_t[i], Buf()) for i in range(3)])

        def rstd_from_psum(pst, PSB, n_feat):
            rt, RB = rsring.next()
            P.op("act", lambda e: e.activation(out=rt[:], in_=pst[:], func=AF.Ln, scale=1.0 / n_feat, bias=epsb[:, 0:1]),
                 reads=[PSB, CONST], writes=[RB])
            P.op("act", lambda e: e.activation(out=rt[:], in_=rt[:], func=AF.Exp, scale=-0.5), reads=[RB], writes=[RB])
            return rt, RB

        def prenorm(t0, gi, layer, hT, HB, hoff, sqbuf=None):
            ti = t0 // 512
            sq, SQB = sqbuf
            xs = [XT[c][ti] for c in range(8)]
            P.op("act", lambda e: e.activation(out=sq[:], in_=xT[:, :, t0:t0 + 512], func=AF.Square), reads=xs, writes=[SQB])
            pst, PSB = psring.next()
            for c in range(8):
                P.op("pe", lambda e, c=c: e.matmul(pst[:], lhsT=ones_b[:], rhs=sq[:, c, :], start=(c == 0), stop=(c == 7)),
                     reads=[SQB, CONST], writes=[PSB])
            rt, RB = rstd_from_psum(pst, PSB, D)
            for c in range(8):
                P.op("dve", lambda e, c=c: e.scalar_tensor_tensor(
                    out=hT[:, c, hoff:hoff + 512], in0=xT[:, c, t0:t0 + 512], scalar=gains[:, gi, layer, c:c + 1],
                    in1=rt[:], op0=ALU.mult, op1=ALU.mult), reads=[XT[c][ti], RB, CONST], writes=[HB])

        def ffn(layer, pools):
            h2, H2B, hid, HIDB, w1r, w2r, sqbuf, YTB, rl_ring = pools
            for tt in range(2):
                T0 = tt * 1024
                for sub in range(2):
                    prenorm(T0 + sub * 512, GN["norm_ffn_pre"], layer, h2, H2B[sub], sub * 512, sqbuf=sqbuf)
                for hg in range(8):
                    w1, W1B = w1r.next()
                    load_w(w1[:], w_ff1[layer].rearrange("(kc p) n -> p kc n", p=128)[:, :, hg * 512:(hg + 1) * 512], W1B)
                    for sub, hh in [(s_, h_) for s_ in range(2) for h_ in range(4)]:
                        hc = hg * 4 + hh
                        if True:
                            pst, PSB = psring.next()
                            for kc in range(8):
                                P.op("pe", lambda e, kc=kc, hh=hh, sub=sub, pst=pst, w1=w1: e.matmul(
                                    pst[:], lhsT=w1[:, kc, hh * 128:(hh + 1) * 128], rhs=h2[:, kc, sub * 512:(sub + 1) * 512],
                                    start=(kc == 0), stop=(kc == 7)), reads=[W1B, H2B[sub]], writes=[PSB])
                            rl, RLB = rl_ring.next()
                            P.op("act", lambda e, pst=pst, rl=rl: e.activation(out=rl[:], in_=pst[:], func=AF.Relu),
                                 reads=[PSB], writes=[RLB])
                            P.op("dve", lambda e, rl=rl, hc=hc, sub=sub: e.tensor_tensor(
                                out=hid[:, hc, sub * 512:(sub + 1) * 512], in0=rl[:], in1=rl[:], op=ALU.mult),
                                reads=[RLB], writes=[HIDB[hc]])
                st_ps = [(ps_t[6], PS[6]), (ps_t[7], PS[7])]
                for oc in range(8):
                    w2, W2B = w2r.next()
                    load_w(w2[:], w_ff2[layer].rearrange("(hc p) n -> p hc n", p=128)[:, :, oc * 128:(oc + 1) * 128], W2B)
                    for sub in range(2):
                        pst, PSB = psring.next()
                        for hc in range(32):
                            P.op("pe", lambda e, hc=hc, sub=sub, pst=pst, w2=w2: e.matmul(
                                pst[:], lhsT=w2[:, hc, :], rhs=hid[:, hc, sub * 512:(sub + 1) * 512],
                                start=(hc == 0), stop=(hc == 31)), reads=[W2B, HIDB[hc]], writes=[PSB])
                        sq, SQB = sqsring.next()
                        P.op("act", lambda e, pst=pst, sq=sq: e.activation(out=sq[:, 0, :], in_=pst[:], func=AF.Square),
                             reads=[PSB], writes=[SQB])
                        P.op("dve", lambda e, pst=pst, oc=oc, sub=sub: e.tensor_copy(
                            out=h2[:, oc, sub * 512:(sub + 1) * 512], in_=pst[:]), reads=[PSB], writes=[H2B[sub]])
                        sp_, SPB = st_ps[sub]
                        P.op("pe", lambda e, sq=sq, sp_=sp_, oc=oc: e.matmul(sp_[:], lhsT=ones_b[:], rhs=sq[:, 0, :],
                                                                      start=(oc == 0), stop=(oc == 7)),
                             reads=[SQB, CONST], writes=[SPB])
                for sub in range(2):
                    sp_, SPB = st_ps[sub]
                    rt, RB = rstd_from_psum(sp_, SPB, D)
                    t0 = T0 + sub * 512
                    ti = t0 // 512
                    for c in range(8):
                        yv = h2[:, c, sub * 512:(sub + 1) * 512]
                        tm, TMB = rl_ring.next()
                        P.op("dve", lambda e, c=c, yv=yv, rt=rt, tm=tm: e.scalar_tensor_tensor(
                            out=tm[:], in0=yv, scalar=gains[:, GN["norm_ffn_post"], layer, c:c + 1], in1=rt[:],
                            op0=ALU.mult, op1=ALU.mult), reads=[H2B[sub], RB, CONST], writes=[TMB])
                        P.op("dve", lambda e, c=c, tm=tm, t0=t0: e.tensor_tensor(
                            out=xT[:, c, t0:t0 + 512], in0=xT[:, c, t0:t0 + 512], in1=tm[:], op=ALU.add),
                            reads=[TMB, XT[c][ti]], writes=[XT[c][ti]])

        TWO_PI = 2.0 * math.pi

        def l0mix(sq_i):
            with ExitStack() as ph:
                def en(n, s, dt=F32):
                    return ph.enter_context(nc.sbuf_tensor("%s_s%d" % (n, sq_i), s, dt)), Buf(n)
                DV = lambda name, reads, writes, **kw: P.I("dve", name, reads=reads, writes=writes, **kw)
                AC = lambda reads, writes, **kw: P.I("act", "activation", reads=reads, writes=writes, **kw)
                MM = lambda reads, writes, **kw: P.I("pe", "matmul", reads=reads, writes=writes, **kw)
                sq, SQB = en("l0sq", [128, 8, 512], BF16)
                cmask, CMB = en("cmask", [128, 512])
                tmask, TMB = en("tmask", [128, 256])
                P.dma(cmask[:], cst_d[:, 128:640], writes=[CMB])
                P.dma(tmask[:], cst_d[:, 640:896], writes=[TMB])
                l0stage, L0S = en("l0stage", [16, 128])
                P.dma(l0stage[0:12, :], lb_table.rearrange("r (h p) -> (r h) p", p=128), writes=[L0S])
                P.dma(l0stage[12:16, :], a_norm.rearrange("(h p) -> h p", p=128), writes=[L0S])
                l0c, L0C = en("l0c", [128, 32])
                pst, PSB = psring.next()
                P.I("pe", "transpose", out=pst[:, 0:16], in_=l0stage[:, :], identity=ident_f[0:16, 0:16], reads=[L0S, CONST], writes=[PSB])
                DV("tensor_copy", [PSB], [L0C], out=l0c[:, 0:16], in_=pst[:, 0:16])
                AC([L0C], [L0C], out=l0c[:, 0:12], in_=l0c[:, 0:12], func=AF.Exp)
                DV("tensor_tensor", [L0C], [L0C], out=l0c[:, 28:32], in0=l0c[:, 0:4], in1=l0c[:, 4:8], op=ALU.add)
                DV("tensor_tensor", [L0C], [L0C], out=l0c[:, 28:32], in0=l0c[:, 28:32], in1=l0c[:, 8:12], op=ALU.add)
                DV("reciprocal", [L0C], [L0C], out=l0c[:, 28:32], in_=l0c[:, 28:32])
                DV("tensor_tensor", [L0C], [L0C], out=l0c[:, 16:20], in0=l0c[:, 0:4], in1=l0c[:, 28:32], op=ALU.mult)
                DV("tensor_scalar", [L0C], [L0C], out=l0c[:, 20:24], in0=l0c[:, 16:20], scalar1=-1.0, scalar2=1.0, op0=ALU.mult, op1=ALU.add)
                DV("tensor_scalar", [L0C], [L0C], out=l0c[:, 24:28], in0=l0c[:, 16:20], scalar1=-1.0, scalar2=None, op0=ALU.add)
                an = lambda hd: l0c[:, 12 + hd:13 + hd]
                lbp = lambda hd: l0c[:, 16 + hd:17 + hd]
                oml = lambda hd: l0c[:, 20 + hd:21 + hd]
                noml = lambda hd: l0c[:, 24 + hd:25 + hd]
                rows, ROWS = en("rows", [1, 1, 512])
                rsrc = (b_ln_g, b_ln_b, b_bias)
                gbc, GBC = en("gbc", [128, 512])
                bbc, BBC = en("bbc", [128, 512])
                biasbc, BIB = en("biasbc", [128, 512])
                for k, (dst, DB) in enumerate(((gbc, GBC), (bbc, BBC), (biasbc, BIB))):
                    P.dma(rows[0:1, 0, :], rsrc[k][:, :], writes=[ROWS])
                    pst, PSB = psring.next()
                    MM([ROWS, CONST], [PSB], out=pst[:], lhsT=ones_f[0:1, :], rhs=rows[0:1, 0, :], start=True, stop=True)
                    DV("tensor_copy", [PSB], [DB], out=dst[:], in_=pst[:])
                wsl, WSL = en("wsl", [128, 4, 128])
                wsT, WST = en("wsT", [128, 4, 128], BF16)
                P.dma(wsl[:], b_ws.rearrange("g t s -> t g s"), writes=[WSL])
                for g in range(4):
                    P.I("pool", "affine_select", reads=[WSL], writes=[WSL], out=wsl[:, g, :], in_=wsl[:, g, :], pattern=[[-1, 128]],
                        compare_op=ALU.is_ge, fill=0.0, base=0, channel_multiplier=1)
                    pst, PSB = psring.next()
                    P.I("pe", "transpose", out=pst[:, 0:128], in_=wsl[:, g, :], identity=ident_f[:], reads=[WSL, CONST], writes=[PSB])
                    DV("tensor_copy", [PSB], [WST], out=wsT[:, g, :], in_=pst[:, 0:128])
                hT, HB = en("hT", [128, 8, 512], BF16)
                wcs = [en("wc%d" % i, [128, 8, 512], BF16) for i in range(2)]
                wring = Ring(wcs)
                mixT, MXB = en("mixT", [128, 8, 512], BF16)
                MXBs = [Buf() for _ in range(8)]
                itok, ITB = en("itok", [128, 4, 512], BF16)
                qs4, QSB = en("qs4", [128, 4, 512], BF16)
                big, _ = en("big", [128, 8, 512])
                sg4, SGB = big[:, 0:4, :], Buf("sg4")
                gate4, GTB = en("gate4", [128, 4, 512], BF16)
                uT, UTB = en("uT", [128, 4, 512], BF16)
                vn, VNB = en("vn", [128, 4, 512], BF16)
                bst, BSTB = en("bst", [128, 8])
                lf, LFB = big[:, 4, :], Buf("lf")
                kk, KKB = big[:, 5, :], Buf("kk")
                vt, VTB = lf, LFB
                vh, VHB = kk, KKB
                G, GB = big[:, 6, :], Buf("G")
                D1, D1B = big[:, 7, :], Buf("D1")
                D3, D3B = D1, D1B
                E1, E1B = en("E1", [128, 512])
                E2, E2B = en("E2", [128, 512])
                E3, E3B = E1, E1B
                qexp, _ = en("qexp", [128, 4, 512], BF16)
                kexp, _ = en("kexp", [128, 4, 512], BF16)
                kdec, _ = en("kdec", [128, 4, 512], BF16)
                kdT, _ = en("kdT", [128, 4, 512], BF16)
                smT, _ = en("smT", [128, 4, 256], BF16)
                eg, _ = en("eg", [128, 4, 16])
                QEBs, KEBs, KDBs, KTBs, SMBs, EGBs = [[Buf() for _ in range(4)] for _ in range(6)]
                Sw, SWB = en("Sw", [128, 9, 128])
                sgs = [en("Sg%d" % i, [128, 8, 128], BF16) for i in range(2)]
                Sc, SCB = en("Sc", [128, 4, 128])
                t1, T1B = en("t1", [128, 512])
                msb, MSB = big, Buf("msb")
                ALIAS = [SGB, LFB, KKB, GB, D1B]
                tm2s = [(t1, T1B), en("tm2b", [128, 512])]
                P.I("pool", "memset", ap=Sc[:], constant=0.0, writes=[SCB])
                w_in_v = w_in_even.rearrange("(kc p) n -> p kc n", p=128)
                w_out_v = w_out_even.rearrange("(kc p) n -> p kc n", p=128)
                G3 = G.rearrange("p (c t) -> p c t", t=64)

                def proj_fm(w, WB, col0, rhsT, RB_):
                    pst, PSB = psring.next()
                    for kc in range(8):
                        MM([WB, RB_], [PSB], out=pst[:], lhsT=w[:, kc, col0:col0 + 128], rhs=rhsT[:, kc, :], start=(kc == 0), stop=(kc == 7))
                    return pst, PSB

                def proj_tm(w, WB, b):
                    pst, PSB = psring.next()
                    for kc in range(8):
                        MM([WB, HB], [PSB], out=pst[:], lhsT=hT[:, kc, b * 128:(b + 1) * 128], rhs=w[:, kc, :], start=(kc == 0), stop=(kc == 7))
                    return pst, PSB

                for ti in range(4):
                    t0 = ti * 512
                    if ti == 0:
                        prenorm(t0, GN["norm_mix_pre"], 0, hT, HB, 0, sqbuf=(sq, SQB))
                    def proj_group(g):
                        w, WB = wring.next()
                        load_w(w[:], w_in_v[:, :, g * 512:(g + 1) * 512], WB)
                        if g in (0, 1, 3, 4):
                            for hd in range(4):
                                pst, PSB = proj_fm(w, WB, hd * 128, hT, HB)
                                if g == 0:
                                    AC([PSB], [QSB], out=qs4[:, hd, :], in_=pst[:], func=AF.Silu)
                                elif g == 1:
                                    AC([PSB], [SGB], out=sg4[:, hd, :], in_=pst[:], func=AF.Sigmoid)
                                elif g == 3:
                                    AC([PSB], [GTB], out=gate4[:, hd, :], in_=pst[:], func=AF.Silu)
                                else:
                                    AC([PSB], [UTB], out=uT[:, hd, :], in_=pst[:], func=AF.Gelu)
                        elif g == 2:
                            for b in range(4):
                                pst, PSB = proj_tm(w, WB, b)
                                AC([PSB], [ITB], out=itok[:, b, :], in_=pst[:], func=AF.Copy)
                        else:
                            for b in range(4):
                                pst, PSB = proj_tm(w, WB, b)
                                AC([PSB], [VTB], out=vt[:], in_=pst[:], func=AF.Gelu)
                                DV("bn_stats", [VTB], [BSTB], out=bst[:, 0:6], in_=vt[:])
                                DV("bn_aggr", [BSTB], [BSTB], out=bst[:, 6:8], in_=bst[:, 0:6])
                                AC([BSTB, CONST], [BSTB], out=bst[:, 7:8], in_=bst[:, 7:8], func=AF.Ln, scale=1.0, bias=epsb[:, 0:1])
                                AC([BSTB], [BSTB], out=bst[:, 7:8], in_=bst[:, 7:8], func=AF.Exp, scale=-0.5)
                                DV("tensor_scalar", [VTB, BSTB], [VHB], out=vh[:], in0=vt[:], scalar1=bst[:, 6:7], scalar2=bst[:, 7:8],
                                   op0=ALU.subtract, op1=ALU.mult)
                                DV("tensor_tensor", [VHB, GBC], [VHB], out=vh[:], in0=vh[:], in1=gbc[:], op=ALU.mult)
                                DV("tensor_tensor", [VHB, BBC], [VNB], out=vn[:, b, :], in0=vh[:], in1=bbc[:], op=ALU.add)
                    def stA(hd):
                        AC([SGB, L0C], [LFB], out=lf[:], in_=sg4[:, hd, :], func=AF.Ln, scale=oml(hd), bias=lbp(hd))
                        DV("tensor_scalar", [SGB, L0C], [KKB], out=kk[:], in0=sg4[:, hd, :], scalar1=noml(hd), scalar2=oml(hd), op0=ALU.mult, op1=ALU.add)
                        DV("tensor_tensor_scan", [LFB, CMB], [GB], out=G[:], data0=cmask[:], data1=lf[:], initial=0.0, op0=ALU.mult, op1=ALU.add)
                        DV("tensor_tensor", [GB], [D1B], out=D1[:].rearrange("p (c t) -> p c t", t=64), in0=G3,
                           in1=G3[:, :, 31:32].to_broadcast([128, 8, 64]), op=ALU.subtract)
                        AC([D1B], [E1B], out=E1[:], in_=D1[:], func=AF.Exp)
                        AC([D1B], [E2B], out=E2[:], in_=D1[:], func=AF.Exp, scale=-1.0)
                        AC([GB], [EGBs[hd]], out=eg[:, hd, 0:8], in_=G3[:, :, 31], func=AF.Exp)
                        AC([GB], [EGBs[hd]], out=eg[:, hd, 8:16], in_=G3[:, :, 63], func=AF.Exp)
                        DV("tensor_tensor", [QSB, E1B], [QEBs[hd]], out=qexp[:, hd, :], in0=qs4[:, hd, :], in1=E1[:], op=ALU.mult)
                        DV("tensor_tensor", [KKB, E2B], [KEBs[hd]], out=kexp[:, hd, :], in0=kk[:], in1=E2[:], op=ALU.mult)
                        DV("tensor_tensor", [GB], [D3B], out=D3[:].rearrange("p (c t) -> p c t", t=64), in0=G3[:, :, 63:64].to_broadcast([128, 8, 64]),
                           in1=G3, op=ALU.subtract)
                        AC([D3B], [E3B], out=E3[:], in_=D3[:], func=AF.Exp)
                        DV("tensor_tensor", [KKB, E3B], [KDBs[hd]], out=kdec[:, hd, :], in0=kk[:], in1=E3[:], op=ALU.mult)

                    def stB(hd):
                        pS, PSS = psring.next()
                        for c in range(8):
                            b, par = c // 2, c % 2
                            MM([KEBs[hd], QEBs[hd]], [PSS], out=pS[par * 64:(par + 1) * 64, b * 64:(b + 1) * 64], lhsT=kexp[:, hd, c * 64:(c + 1) * 64],
                               rhs=qexp[:, hd, c * 64:(c + 1) * 64], start=True, stop=True)
                        DV("tensor_tensor", [PSS, TMB], [SMBs[hd]], out=smT[:, hd, :], in0=pS[:, 0:256], in1=tmask[:], op=ALU.mult)
                        pT_, PTB = psring.next()
                        pTb = pT_[:].bitcast(BF16)
                        for b in range(4):
                            P.I("pe", "transpose", out=pTb[:, b * 128:(b + 1) * 128], in_=kdec[:, hd, b * 128:(b + 1) * 128], identity=ident_b[:],
                                reads=[KDBs[hd], CONST], writes=[PTB])
                        AC([PTB], [KTBs[hd]], out=kdT[:, hd, :], in_=pTb[:, 0:512], func=AF.Copy)
                        DV("tensor_copy", [SCB], [SWB], out=Sw[:, 0, :], in_=Sc[:, hd, :])
                        pUs = [psring.next(), psring.next()]
                        for c in range(8):
                            b, par = c // 2, c % 2
                            pU, PUB = pUs[par]
                            MM([KTBs[hd], ITB], [PUB], out=pU[:, b * 128:(b + 1) * 128], lhsT=kdT[par * 64:(par + 1) * 64, hd, b * 128:(b + 1) * 128],
                               rhs=itok[par * 64:(par + 1) * 64, b, hd * 128:(hd + 1) * 128], start=True, stop=True)
                        for c in range(8):
                            b, par = c // 2, c % 2
                            pU, PUB = pUs[par]
                            DV("scalar_tensor_tensor", [SWB, EGBs[hd], PUB], [SWB], out=Sw[:, c + 1, :], in0=Sw[:, c, :], scalar=eg[:, hd, 8 + c:9 + c],
                               in1=pU[:, b * 128:(b + 1) * 128], op0=ALU.mult, op1=ALU.add)
                        DV("tensor_copy", [SWB], [SCB], out=Sc[:, hd, :], in_=Sw[:, 8, :])
                        Sg, SGGB = sgs[hd % 2]
                        DV("tensor_tensor", [SWB, EGBs[hd]], [SGGB], out=Sg[:], in0=Sw[:, 0:8, :], in1=eg[:, hd, 0:8].unsqueeze(2).to_broadcast([128, 8, 128]), op=ALU.mult)

                    def stC(hd):
                        Sg, SGGB = sgs[hd % 2]
                        pO, POB = psring.next()
                        for c in range(8):
                            b, par = c // 2, c % 2
                            MM([SGGB, QEBs[hd]], [POB], out=pO[:, c * 64:(c + 1) * 64], lhsT=Sg[:, c, :], rhs=qexp[:, hd, c * 64:(c + 1) * 64], start=True, stop=False)
                            MM([ITB, SMBs[hd]], [POB], out=pO[:, c * 64:(c + 1) * 64], lhsT=itok[par * 64:(par + 1) * 64, b, hd * 128:(hd + 1) * 128],
                               rhs=smT[par * 64:(par + 1) * 64, hd, b * 64:(b + 1) * 64], start=False, stop=True)
                        sqs, SQSB = sqsring.next()
                        AC([POB], [SQSB], out=sqs[:, 0, :], in_=pO[:], func=AF.Square)
                        pN, PNB = psring.next()
                        MM([SQSB, CONST], [PNB], out=pN[:], lhsT=ones_b[:], rhs=sqs[:, 0, :], start=True, stop=True)
                        rt, RB = rstd_from_psum(pN, PNB, 128)
                        DV("scalar_tensor_tensor", [POB, RB, L0C], [T1B], out=t1[:], in0=pO[:], scalar=an(hd), in1=rt[:], op0=ALU.mult, op1=ALU.mult)
                        DV("tensor_tensor", [T1B, GTB], [MXBs[hd]], out=mixT[:, hd, :], in0=t1[:], in1=gate4[:, hd, :], op=ALU.mult)

                    L0ORD = {0: "G0 G1 G2 G3 G4 G5 A0 A1 B0 A2 B1 C0 A3 B2 C1 B3 C2 C3",
                             1: "G0 G1 G5 A0 G2 A1 B0 G3 A2 B1 C0 G4 A3 B2 C1 B3 C2 C3"}[cfg.get("l0ord", 0)]
                    for step in [(w[0], int(w[1])) for w in L0ORD.split()]:
                        {"A": stA, "B": stB, "C": stC, "G": proj_group}[step[0]](step[1])
                    for g in range(4):
                        pM, PMB = psring.next()
                        for b in range(4):
                            MM([VNB, WST], [PMB], out=pM[:, b * 128:(b + 1) * 128], lhsT=vn[:, b, g * 128:(g + 1) * 128], rhs=wsT[:, g, :], start=True, stop=True)
                        DV("tensor_tensor", [PMB, BIB], [T1B], out=t1[:].rearrange("p (b t) -> p b t", t=128), in0=pM[:].rearrange("p (b t) -> p b t", t=128),
                           in1=biasbc[:, g * 128:(g + 1) * 128].unsqueeze(1).to_broadcast([128, 4, 128]), op=ALU.add)
                        DV("tensor_tensor", [T1B, UTB], [MXBs[4 + g]], out=mixT[:, 4 + g, :], in0=t1[:], in1=uT[:, g, :], op=ALU.mult)
                    if ti < 3:
                        prenorm(t0 + 512, GN["norm_mix_pre"], 0, hT, HB, 0, sqbuf=(sq, SQB))
                    outproj(w_out_v, wring, mixT, MXBs, msb, MSB, tm2s, None, 0, t0, alias=ALIAS)
                P.barrier()

        def outproj(w_out_v, wring, mixT, MXB, msb, MSB, tm2, TM2B, layer, t0, moff=0, alias=()):
            ti = t0 // 512
            sp_, SPB = ps_t[6], PS[6]
            for g in range(2):
                w, WB = wring.next()
                load_w(w[:], w_out_v[:, :, g * 512:(g + 1) * 512], WB)
                for j in range(4):
                    oc = g * 4 + j
                    pst, PSB = psring.next()
                    for kc in range(8):
                        P.I("pe", "matmul", reads=[WB] + (MXB if isinstance(MXB, list) else [MXB]), writes=[PSB], out=pst[:], lhsT=w[:, kc, j * 128:(j + 1) * 128],
                            rhs=mixT[:, kc, moff:moff + 512], start=(kc == 0), stop=(kc == 7))
                    sqs, SQSB = sqsring.next()
                    P.I("act", "activation", reads=[PSB], writes=[SQSB], out=sqs[:, 0, :], in_=pst[:], func=AF.Square)
                    P.I("dve", "tensor_copy", reads=[PSB], writes=[MSB] + list(alias), out=msb[:, oc, :], in_=pst[:])
                    P.I("pe", "matmul", reads=[SQSB, CONST], writes=[SPB], out=sp_[:], lhsT=ones_b[:], rhs=sqs[:, 0, :], start=(oc == 0), stop=(oc == 7))
            rt, RB = rstd_from_psum(sp_, SPB, D)
            tms = tm2 if isinstance(tm2, list) else [(tm2, TM2B)]
            for c in range(8):
                tm_, TMB_ = tms[c % len(tms)]
                P.I("dve", "scalar_tensor_tensor", reads=[MSB, RB, CONST] + list(alias), writes=[TMB_], out=tm_[:], in0=msb[:, c, :],
                    scalar=gains[:, GN["norm_mix_post"], layer, c:c + 1], in1=rt[:], op0=ALU.mult, op1=ALU.mult)
                P.I("dve", "tensor_tensor", reads=[TMB_, XT[c][ti]], writes=[XT[c][ti]], out=xT[:, c, t0:t0 + 512], in0=xT[:, c, t0:t0 + 512],
                    in1=tm_[:], op=ALU.add)

        def l1mix(sq_i):
            r0 = sq_i * S
            DV = lambda name, reads, writes, **kw: P.I("dve", name, reads=reads, writes=writes, **kw)
            AC = lambda reads, writes, **kw: P.I("act", "activation", reads=reads, writes=writes, **kw)
            MM = lambda reads, writes, **kw: P.I("pe", "matmul", reads=reads, writes=writes, **kw)
            with ExitStack() as po:
                def eno(n, s, dt=F32):
                    return po.enter_context(nc.sbuf_tensor("%s_s%d" % (n, sq_i), s, dt)), Buf(n)
                hT, HB = eno("hT1", [128, 8, S], BF16)
                oT, OTB = eno("oT1", [128, 8, S], BF16)
                OTBs = [Buf() for _ in range(8)]
                with ExitStack() as ph:
                    sq = ph.enter_context(nc.sbuf_tensor("l1sq_s%d" % sq_i, [128, 8, 512], BF16))
                    SQB = Buf()
                    for ti in range(4):
                        prenorm(ti * 512, GN["norm_mix_pre"], 1, hT, HB, ti * 512, sqbuf=(sq, SQB))
                P.barrier()
                with ExitStack() as ph:
                    def en(n, s, dt=F32):
                        return ph.enter_context(nc.sbuf_tensor("%s_s%d" % (n, sq_i), s, dt)), Buf(n)
                    qkT, QKB = en("qkT", [128, 6, S], BF16)
                    P.I("pool", "memset", ap=qkT[64:128, 2:4, :], constant=0.0, writes=[QKB])
                    P.I("pool", "memset", ap=qkT[0:64, 4:6, :], constant=0.0, writes=[QKB])
                    Vt, VB = en("Vt", [128, 16, 4, 128], BF16)
                    wraw, WQB = en("wraw", [128, 8 * 768], BF16)
                    wqk = wraw[:].rearrange("p (k n) -> p k n", n=768)
                    qtbs = [en("qtb%d" % i, [128, 512], BF16) for i in range(2)]
                    qtring = Ring(qtbs)
                    pTs = [(wraw[:, i * 512:(i + 1) * 512], Buf("pt%d" % i)) for i in range(4)] + \
                          [(wraw[:, 4096 + i * 512:4096 + (i + 1) * 512], Buf("pt%d" % (4 + i))) for i in range(3)]
                    ptring = Ring(pTs)
                    evs = [(wraw[:, 2048 + i * 1024:2048 + (i + 1) * 1024].bitcast(F32), Buf("ev%d" % i)) for i in range(2)]
                    evring = Ring(evs)
                    ALIASW = [b for _, b in pTs] + [b for _, b in evs]
                    swapm, SWB_ = en("swapm", [128, 128])
                    am, AMB = en("am", [128, 9, 512], BF16)
                    cs, CSB = en("cs", [128, 16, 8])
                    sn, SNB = en("sn", [128, 16, 8])
                    rtmp, RTB = en("rtmp", [128, 4, 8, 8])
                    RTBs = [Buf() for _ in range(4)]
                    load_w(am[:], am_d[:, :, :], AMB)
                    P.I("pool", "memset", ap=Vt[:, :, 0::2, 64:128], constant=1.0, writes=[VB])
                    P.I("pool", "memset", ap=Vt[:, :, 1::2, 0:64], constant=1.0, writes=[VB])
                    P.I("pool", "tensor_copy", out=swapm[:, 0:64], in_=ident_f[:, 64:128], reads=[CONST], writes=[SWB_])
                    P.I("pool", "tensor_copy", out=swapm[:, 64:128], in_=ident_f[:, 0:64], reads=[CONST], writes=[SWB_])
                    ki, KIB = en("ki", [128, 128], I32)
                    kf, KFB = en("kf", [128, 128])
                    posr, PRB = ki[0:16, :], KIB
                    posf, PFB = kf[0:16, :], KFB
                    posT, PTB_ = en("posT", [128, 16])
                    invf, IVB = en("invf", [128, 8])
                    ang, ANB = en("ang", [128, 128])
                    a2, A2B = en("a2", [128, 128])
                    mk_, MKB = kf, KFB
                    P.dma(invf[:], cst_d[:, 896:904], writes=[IVB])
                    P.dma(posr[:], pos_d[r0:r0 + S].rearrange("(b p) -> b p", p=128), writes=[PRB])
                    DV("tensor_copy", [PRB], [PFB], out=posf[:], in_=posr[:])
                    pst, PSB = psring.next()
                    P.I("pe", "transpose", out=pst[:, 0:16], in_=posf[:, :], identity=ident_f[0:16, 0:16], reads=[PFB, CONST], writes=[PSB])
                    DV("tensor_copy", [PSB], [PTB_], out=posT[:], in_=pst[:, 0:16])
                    DV("tensor_tensor", [PTB_, IVB], [ANB], out=ang[:].rearrange("p (b j) -> p b j", j=8), in0=posT[:].unsqueeze(2).to_broadcast([128, 16, 8]),
                       in1=invf[:].unsqueeze(1).to_broadcast([128, 16, 8]), op=ALU.mult)

                    def sin_of(shift, dst, DB):
                        DV("tensor_scalar", [ANB], [A2B], out=a2[:], in0=ang[:], scalar1=float(shift), scalar2=None, op0=ALU.add)
                        DV("tensor_scalar", [A2B], [KFB], out=kf[:], in0=a2[:], scalar1=1.0 / TWO_PI, scalar2=None, op0=ALU.mult)
                        DV("tensor_copy", [KFB], [KIB], out=ki[:], in_=kf[:])
                        DV("tensor_copy", [KIB], [KFB], out=kf[:], in_=ki[:])
                        DV("scalar_tensor_tensor", [KFB, A2B], [A2B], out=a2[:], in0=kf[:], scalar=-TWO_PI, in1=a2[:], op0=ALU.mult, op1=ALU.add)
                        DV("tensor_scalar", [A2B], [MKB], out=mk_[:], in0=a2[:], scalar1=math.pi, scalar2=None, op0=ALU.is_gt)
                        DV("scalar_tensor_tensor", [MKB, A2B], [A2B], out=a2[:], in0=mk_[:], scalar=-TWO_PI, in1=a2[:], op0=ALU.mult, op1=ALU.add)
                        DV("tensor_scalar", [A2B], [MKB], out=mk_[:], in0=a2[:], scalar1=-math.pi, scalar2=None, op0=ALU.is_lt)
                        DV("scalar_tensor_tensor", [MKB, A2B], [A2B], out=a2[:], in0=mk_[:], scalar=TWO_PI, in1=a2[:], op0=ALU.mult, op1=ALU.add)
                        DV("tensor_scalar", [A2B], [A2B], out=a2[:], in0=a2[:], scalar1=math.pi, scalar2=-math.pi, op0=ALU.min, op1=ALU.max)
                        AC([A2B], [DB], out=dst[:].rearrange("p b j -> p (b j)"), in_=a2[:], func=AF.Sin)
                    sin_of(0.0, sn, SNB)
                    sin_of(math.pi / 2, cs, CSB)

                    w_v = w_in_odd.rearrange("(kc p) n -> p kc n", p=128)
                    poring = Ring([(ps_t[i], PS[i]) for i in (4, 5, 6, 7)])
                    for qt in range(4):
                        for part in range(3):
                            P.dma(wqk[:, :, part * 256:(part + 1) * 256], w_v[:, :, part * 1024 + qt * 256: part * 1024 + (qt + 1) * 256],
                                  writes=[WQB] + ALIASW, eng=wdma_eng)
                        def proj_a(tb):
                            pA, PAB = psring.next()
                            for kc in range(8):
                                MM([WQB, HB], [PAB], out=pA[:], lhsT=hT[:, kc, tb * 128:(tb + 1) * 128], rhs=wqk[:, kc, 0:512], start=(kc == 0), stop=(kc == 7))
                            pB, PBB = psring.next()
                            for kc in range(8):
                                MM([WQB, HB], [PBB], out=pB[:, 0:256], lhsT=hT[:, kc, tb * 128:(tb + 1) * 128], rhs=wqk[:, kc, 512:768], start=(kc == 0), stop=(kc == 7))
                            qtb, QTB = qtring.next()
                            AC([PAB], [QTB], out=qtb[:], in_=pA[:], func=AF.Copy)
                            pA3 = pA[:].rearrange("p (h d) -> p h d", d=64)
                            q3 = qtb[:].rearrange("p (h d) -> p h d", d=64)
                            cb = cs[:, tb, :].unsqueeze(1).to_broadcast([128, 8, 8])
                            sb_ = sn[:, tb, :].unsqueeze(1).to_broadcast([128, 8, 8])
                            DV("tensor_tensor", [PAB, CSB], [RTBs[0]], out=rtmp[:, 0, :, :], in0=pA3[:, :, 0:8], in1=cb, op=ALU.mult)
                            DV("tensor_tensor", [PAB, SNB], [RTBs[1]], out=rtmp[:, 1, :, :], in0=pA3[:, :, 8:16], in1=sb_, op=ALU.mult)
                            DV("tensor_tensor", [PAB, SNB], [RTBs[2]], out=rtmp[:, 2, :, :], in0=pA3[:, :, 0:8], in1=sb_, op=ALU.mult)
                            DV("tensor_tensor", [PAB, CSB], [RTBs[3]], out=rtmp[:, 3, :, :], in0=pA3[:, :, 8:16], in1=cb, op=ALU.mult)
                            DV("tensor_tensor", [RTBs[0], RTBs[1], QTB], [QTB], out=q3[:, :, 0:8], in0=rtmp[:, 0, :, :], in1=rtmp[:, 1, :, :], op=ALU.subtract)
                            DV("tensor_tensor", [RTBs[2], RTBs[3], QTB], [QTB], out=q3[:, :, 8:16], in0=rtmp[:, 2, :, :], in1=rtmp[:, 3, :, :], op=ALU.add)
                            pB3 = pB[:, 0:256].rearrange("p (h d) -> p h d", d=64)
                            AC([PBB], [VB], out=Vt[:, tb, 0::2, 0:64], in_=pB3[:, 0::2, :], func=AF.Copy)
                            AC([PBB], [VB], out=Vt[:, tb, 1::2, 64:128], in_=pB3[:, 1::2, :], func=AF.Copy)
                            return qtb, QTB

                        def proj_b(tb, qtb, QTB):
                            pT_, PTB = psring.next()
                            pTb = pT_[:].bitcast(BF16)
                            for j in range(4):
                                P.I("pe", "transpose", out=pTb[:, j * 128:(j + 1) * 128], in_=qtb[:, j * 128:(j + 1) * 128], identity=ident_b[:],
                                    reads=[QTB, CONST], writes=[PTB])
                            DV("tensor_copy", [PTB], [QKB], out=qkT[:, 0:2, tb * 128:(tb + 1) * 128], in_=pTb[:, 0:256].rearrange("p (j t) -> p j t", t=128))
                            DV("tensor_copy", [PTB], [QKB], out=qkT[0:64, 2:4, tb * 128:(tb + 1) * 128], in_=pTb[0:64, 256:512].rearrange("p (j t) -> p j t", t=128))
                            DV("tensor_copy", [PTB], [QKB], out=qkT[64:128, 4:6, tb * 128:(tb + 1) * 128], in_=pTb[64:128, 256:512].rearrange("p (j t) -> p j t", t=128))

                        prev = None
                        for tb in range(16):
                            cur = proj_a(tb)
                            if prev is not None:
                                proj_b(*prev)
                            prev = (tb,) + cur
                        proj_b(*prev)
                        nh = 4 if cfg.get("l1stage", 9) > 1 else 0
                        items = [(2 * pr + e, T, kb) for pr in range(nh // 2) for T in range(4) for kb in range(4 * T + 4) for e in range(2)]
                        PRE = 3
                        qk_state, pO_of, deferred = {}, {}, []
                        qring = Ring([(ps_t[i], PS[i]) for i in range(4)])

                        def emit_qk(i):
                            lh, T, kb = items[i]
                            j, po_ = lh // 2, (lh % 2) * 64
                            col0 = max(0, kb - 4 * T) * 128
                            pS, PSS = qring.next()
                            MM([QKB], [PSS], out=pS[:, col0:512], lhsT=qkT[:, (2 if po_ == 0 else 4) + j, kb * 128:(kb + 1) * 128],
                               rhs=qkT[:, j, T * 512 + col0:(T + 1) * 512], start=True, stop=True)
                            qk_state[i] = (pS, PSS, col0)

                        def emit_rest(i):
                            lh, T, kb = items[i]
                            hglob = qt * 4 + lh
                            nkb = 4 * T + 4
                            rel = 4 * T - kb
                            midx = rel + 3 if rel <= 4 else 8
                            if kb == 0:
                                pO_of[(lh, T)] = poring.next()
                            pO, POB = pO_of[(lh, T)]
                            pS, PSS, col0 = qk_state.pop(i)
                            pt, PTT = ptring.next()
                            AC([PSS], [PTT, WQB], out=pt[:, col0:512], in_=pS[:, col0:512], func=AF.Exp, scale=0.125)
                            DV("tensor_tensor", [PTT, AMB], [PTT], out=pt[:, col0:512], in0=pt[:, col0:512], in1=am[:, midx, col0:512], op=ALU.mult)
                            MM([VB, PTT], [POB], out=pO[:, col0:512], lhsT=Vt[:, kb, lh, :], rhs=pt[:, col0:512], start=(kb == 0), stop=(kb == nkb - 1))
                            if kb == nkb - 1 and lh % 2 == 1:
                                pOA, POA = pO_of[(lh - 1, T)]
                                pOB_, POBB = pO, POB
                                num, NUMB = evring.next()
                                zz, ZB = evring.next()

                                def ep1(pOA=pOA, POA=POA, pOB_=pOB_, POBB=POBB, num=num, NUMB=NUMB, zz=zz, ZB=ZB):
                                    AC([POA], [NUMB, WQB], out=num[0:64, :], in_=pOA[0:64, :], func=AF.Copy)
                                    AC([POA], [ZB, WQB], out=zz[64:128, :], in_=pOA[64:128, :], func=AF.Ln)
                                    AC([POBB], [NUMB], out=num[64:128, :], in_=pOB_[64:128, :], func=AF.Copy)
                                    AC([POBB], [ZB], out=zz[0:64, :], in_=pOB_[0:64, :], func=AF.Ln)
                                    AC([ZB], [ZB], out=zz[:, :], in_=zz[:, :], func=AF.Exp, scale=-1.0)

                                def ep2(num=num, NUMB=NUMB, zz=zz, ZB=ZB, hglob=hglob, T=T, pOA=pOA, POA=POA):
                                    pSw, PSWB = pOA, POA
                                    MM([ZB, SWB_], [PSWB], out=pSw[:], lhsT=swapm[:], rhs=zz[:, :], start=True, stop=True)
                                    DV("tensor_tensor", [NUMB, PSWB], [OTBs[hglob // 2]], out=oT[:, hglob // 2, T * 512:(T + 1) * 512],
                                       in0=num[:, :], in1=pSw[:], op=ALU.mult)
                                deferred.append((i + 1, ep1))
                                deferred.append((i + 4, ep2))

                        for i in range(len(items) + PRE + 8):
                            if i < len(items):
                                emit_qk(i)
                            jx = i - PRE
                            if 0 <= jx < len(items):
                                emit_rest(jx)
                            due = [d for d in deferred if d[0] <= jx]
                            for d in due:
                                deferred.remove(d)
                                d[1]()
                        assert not deferred
                P.barrier()
                with ExitStack() as ph:
                    def en(n, s, dt=F32):
                        return ph.enter_context(nc.sbuf_tensor("%s_s%d" % (n, sq_i), s, dt)), Buf(n)
                    wcs = [en("wo%d" % i, [128, 8, 512], BF16) for i in range(2)]
                    wring = Ring(wcs)
                    msb, MSB = en("msb1", [128, 8, 512])
                    tm2s = [en("tm21", [128, 512]), en("tm22", [128, 512])]
                    w_out_v = w_out_odd.rearrange("(kc p) n -> p kc n", p=128)
                    for ti in range(4):
                        outproj(w_out_v, wring, oT, OTBs, msb, MSB, tm2s, None, 1, ti * 512, moff=ti * 512)
            P.barrier()

        for sq_i in range(nseq):
            r0 = sq_i * S
            with ExitStack() as ph:
                xin = [ph.enter_context(nc.sbuf_tensor("xin%d_%d" % (sq_i, i), [128, D], F32)) for i in range(3)]
                xring = Ring([(xin[i], Buf()) for i in range(3)])
                for tb in range(16):
                    xi, XIB = xring.next()
                    P.dma(xi[:], x_d[r0 + tb * 128: r0 + (tb + 1) * 128, :], writes=[XIB])
                    for half in range(2):
                        pst, PSB = psring.next()
                        for j in range(4):
                            c = half * 4 + j
                            P.op("pe", lambda e, c=c, j=j, pst=pst, xi=xi: e.transpose(
                                out=pst[:, j * 128:(j + 1) * 128], in_=xi[:, c * 128:(c + 1) * 128], identity=ident_f[:]),
                                reads=[XIB, CONST], writes=[PSB])
                        eng = "act" if half == 0 else "dve"
                        dst = xT[:, half * 4:(half + 1) * 4, tb * 128:(tb + 1) * 128]
                        src = pst[:].rearrange("p (j t) -> p j t", t=128)
                        wr = [XT[half * 4 + j][tb // 4] for j in range(4)]
                        if eng == "act":
                            P.op("act", lambda e, dst=dst, src=src: e.activation(out=dst, in_=src, func=AF.Copy), reads=[PSB], writes=wr)
                        else:
                            P.op("dve", lambda e, dst=dst, src=src: e.tensor_copy(out=dst, in_=src), reads=[PSB], writes=wr)

            P.barrier()
            def run_ffn(layer):
                with ExitStack() as ph:
                    en = lambda n, s, dt=F32: ph.enter_context(nc.sbuf_tensor("%s_s%d_l%d" % (n, sq_i, layer), s, dt))
                    h2 = en("h2", [128, 8, 1024], BF16)
                    hid = en("hid", [128, 32, 1024], BF16)
                    w1 = [en("w1_%d" % i, [128, 8, 512], BF16) for i in range(2)]
                    w2 = [en("w2_%d" % i, [128, 32, 128], BF16) for i in range(2)]
                    rl = [en("rl%d" % i, [128, 512], F32) for i in range(3)]
                    fsq = en("fsq", [128, 8, 512], BF16)
                    pools = (h2, [Buf(), Buf()], hid, [Buf() for _ in range(32)],
                             Ring([(w1[i], Buf()) for i in range(2)]), Ring([(w2[i], Buf()) for i in range(2)]),
                             (fsq, Buf()), None, Ring([(rl[i], Buf()) for i in range(3)]))
                    ffn(layer, pools)
                P.barrier()

            if do_l0mix:
                l0mix(sq_i)
            if do_l0ffn:
                run_ffn(0)
            if do_l1mix:
                l1mix(sq_i)
            if do_l1ffn:
                run_ffn(1)

            with ExitStack() as ph:
                xo = [ph.enter_context(nc.sbuf_tensor("xo%d_%d" % (sq_i, i), [128, D], F32)) for i in range(3)]
                oring = Ring([(xo[i], Buf()) for i in range(3)])
                for tb in range(16):
                    xo_, XOB = oring.next()
                    for half in range(2):
                        pst, PSB = psring.next()
                        for j in range(4):
                            c = half * 4 + j
                            P.I("pe", "transpose", out=pst[:, j * 128:(j + 1) * 128], in_=xT[:, c, tb * 128:(tb + 1) * 128],
                                identity=ident_f[:], reads=[XT[c][tb // 4], CONST], writes=[PSB])
                        dst = xo_[:, half * 512:(half + 1) * 512]
                        if half == 0:
                            P.op("act", lambda e, dst=dst, pst=pst: e.activation(out=dst, in_=pst[:], func=AF.Copy), reads=[PSB], writes=[XOB])
                        else:
                            P.op("dve", lambda e, dst=dst, pst=pst: e.tensor_copy(out=dst, in_=pst[:]), reads=[PSB], writes=[XOB])
                    P.dma(out_d[r0 + tb * 128: r0 + (tb + 1) * 128, :], xo_[:], reads=[XOB], is_output=True)
            P.barrier()

        P.emit()
    return nc


_INPUT_ORDER = ["x", "positions", "norm_mix_pre", "norm_mix_post", "norm_ffn_pre", "norm_ffn_post",
                "w_in_even", "lb_table", "a_norm", "b_ln_g", "b_ln_b", "b_ws", "b_bias", "w_out_even",
                "w_in_odd", "w_out_odd", "w_ff1", "w_ff2"]


def host_tables():
    cst = np.zeros((128, 904), np.float32)
    cst[:, 0:128] = np.eye(128, dtype=np.float32)
    cm = np.ones((128, 512), np.float32)
    cm[:, ::64] = 0.0
    cst[:, 128:640] = cm
    p = np.arange(128)[:, None] % 64
    t = np.arange(256)[None, :] % 64
    cst[:, 640:896] = (p <= t).astype(np.float32)
    half = 8
    cst[:, 896:904] = (ROPE_THETA ** (-np.arange(half, dtype=np.float32) / half)).astype(np.float32)[None, :]
    am = np.zeros((128, 9, 512), np.float32)
    i = np.arange(128)[:, None]
    jj = np.arange(512)[None, :]
    for m in range(9):
        rel = m - 3 if m < 8 else 5
        dl = 128 * rel + jj - i
        c = ((dl >= 0) & (dl <= 128)).astype(np.float32) + ((dl >= 0) & (dl <= 512) & (dl % 4 == 0)).astype(np.float32) \
            + ((dl >= 0) & (dl <= 2048) & (dl % 16 == 0)).astype(np.float32)
        am[:, m, :] = c
    return cst, am


def make_in_maps(inputs, n_cores=NC8, nseq=2):
    f = lambda a: np.ascontiguousarray(np.asarray(a))
    x = f(inputs["x"]).astype(np.float32, copy=False)
    pos = f(inputs["positions"]).astype(np.int32, copy=False)
    shared = {
        "norm_mix_pre": f(inputs["norm_mix_pre"]), "norm_mix_post": f(inputs["norm_mix_post"]),
        "norm_ffn_pre": f(inputs["norm_ffn_pre"]), "norm_ffn_post": f(inputs["norm_ffn_post"]),
        "w_in_even": f(inputs["w_in_even"]).reshape(D, 3072), "lb_table": f(inputs["lb_table"]),
        "a_norm": f(inputs["a_norm"]).reshape(512), "b_ln_g": f(inputs["b_ln_g"]).reshape(1, 512),
        "b_ln_b": f(inputs["b_ln_b"]).reshape(1, 512), "b_ws": f(inputs["b_ws"]).reshape(4, 128, 128),
        "b_bias": f(inputs["b_bias"]).reshape(1, 512), "w_out_even": f(inputs["w_out_even"]).reshape(D, D),
        "w_in_odd": f(inputs["w_in_odd"]).reshape(D, 3072), "w_out_odd": f(inputs["w_out_odd"]).reshape(D, D),
        "w_ff1": f(inputs["w_ff1"]), "w_ff2": f(inputs["w_ff2"]),
    }
    shared["cst"], shared["amask"] = host_tables()
    maps = []
    for c in range(n_cores):
        m = dict(shared)
        m["x"] = x[c * nseq:(c + 1) * nseq].reshape(nseq * S, D)
        m["positions"] = pos[c * nseq:(c + 1) * nseq].reshape(nseq * S)
        maps.append(m)
    return maps


def kernel(**inputs):
    nc = build(nseq=2)
    maps = make_in_maps(inputs)
    res = run_bass_kernel_spmd(nc, maps, core_ids=list(range(NC8)))
    outs = [np.asarray(r["out"]).reshape(2, S, D) for r in res.results]
    return np.concatenate(outs, axis=0).astype(np.float32, copy=False)
```
